# Optimizing a Trainium2 kernel written in Bass

```python
import math
import jax, jax.numpy as jnp
from jax import lax
import numpy as np

D_MODEL = 1024
BATCH = 8
SEQ = 4096
DEPTH = 1

CHUNK = 64
LEFT_CHUNKS = 8
BAND = LEFT_CHUNKS + 1
D_CONV = D_MODEL // 2
CONV_WIDTH = 31
N_HEADS = 8
HEAD_DIM = 64
D_ATTN = N_HEADS * HEAD_DIM
MAX_REL = 128
N_REL = 2 * MAX_REL + 1
LN_EPS = 1e-5
DEEPNORM_ALPHA = (2.0 * DEPTH) ** 0.25
DEEPNORM_BETA = (8.0 * DEPTH) ** -0.25

COL_SPLITS = [
    D_CONV,
    D_CONV,
    D_CONV,
    D_ATTN,
    D_ATTN,
    D_ATTN,
    D_ATTN,
    D_MODEL,
    D_MODEL,
]
D_IN = sum(COL_SPLITS)

kernel_name = "hybrid_conformer_conv_chunk_attn_deepnorm"


def _layer_norm(x, g, b):
    xf = x.astype(jnp.float32)
    mu = jnp.mean(xf, axis=-1, keepdims=True)
    var = jnp.mean(jnp.square(xf - mu), axis=-1, keepdims=True)
    y = (xf - mu) * lax.rsqrt(var + LN_EPS) * g.astype(jnp.float32) + b.astype(jnp.float32)
    return y.astype(x.dtype)


def _causal_depthwise_conv(u, w, b):
    c = u.shape[-1]
    y = lax.conv_general_dilated(
        u, w.reshape(CONV_WIDTH, 1, c).astype(u.dtype),
        window_strides=(1,), padding=((CONV_WIDTH - 1, 0),),
        dimension_numbers=("NWC", "WIO", "NWC"), feature_group_count=c)
    return y + b


def _chunked_attention(q, k, v, rel_bias):
    bsz, seq, _ = q.shape
    n_chunks = seq // CHUNK
    q = q.reshape(bsz, n_chunks, CHUNK, N_HEADS, HEAD_DIM)
    k = k.reshape(bsz, n_chunks, CHUNK, N_HEADS, HEAD_DIM)
    v = v.reshape(bsz, n_chunks, CHUNK, N_HEADS, HEAD_DIM)
    pad = ((0, 0), (LEFT_CHUNKS, 0), (0, 0), (0, 0), (0, 0))
    kp = jnp.pad(k, pad)
    vp = jnp.pad(v, pad)
    kb = jnp.concatenate([kp[:, w:w + n_chunks] for w in range(BAND)], axis=2)
    vb = jnp.concatenate([vp[:, w:w + n_chunks] for w in range(BAND)], axis=2)

    scale = 1.0 / math.sqrt(HEAD_DIM)
    s = jnp.einsum("bnqhd,bnkhd->bhnqk", q, kb).astype(jnp.float32) * scale

    qi = np.arange(CHUNK)[:, None]
    kj = np.arange(BAND * CHUNK)[None, :]
    rel = np.clip(LEFT_CHUNKS * CHUNK + qi - kj, -MAX_REL, MAX_REL) + MAX_REL
    bias = rel_bias.astype(jnp.float32)[:, rel]
    s = s + bias[:, None]

    key_chunk = np.arange(n_chunks)[:, None] - LEFT_CHUNKS + (np.arange(BAND * CHUNK) // CHUNK)[None, :]
    valid = jnp.asarray(key_chunk >= 0)
    s = jnp.where(valid[:, None, :], s, -1e30)
    p = jax.nn.softmax(s, axis=-1).astype(v.dtype)
    o = jnp.einsum("bhnqk,bnkhd->bnqhd", p, vb)
    return o.reshape(bsz, seq, D_ATTN)


def setup_inputs(seed: int = 0) -> dict:
    key = jax.random.key(seed)
    ks = jax.random.split(key, 16)
    f32 = jnp.float32
    x = jax.random.normal(ks[0], (BATCH, SEQ, D_MODEL), f32)

    w_in = jax.random.normal(ks[1], (D_MODEL, D_IN), f32) * D_MODEL ** -0.5
    v_start = 3 * D_CONV + 2 * D_ATTN
    col_scale = jnp.ones((D_IN,), f32).at[v_start:v_start + D_ATTN].set(DEEPNORM_BETA)
    w_in = w_in * col_scale
    b_in = jax.random.normal(ks[2], (D_IN,), f32) * 0.02

    conv_w = jax.random.normal(ks[3], (CONV_WIDTH, D_CONV), f32) * CONV_WIDTH ** -0.5
    conv_b = jax.random.normal(ks[4], (D_CONV,), f32) * 0.02
    conv_ln_g = 1.0 + 0.05 * jax.random.normal(ks[5], (D_CONV,), f32)
    conv_ln_b = 0.02 * jax.random.normal(ks[6], (D_CONV,), f32)
    w_conv_out = jax.random.normal(ks[7], (D_CONV, D_MODEL), f32) * D_CONV ** -0.5 * DEEPNORM_BETA

    rel_bias = 0.2 * jax.random.normal(ks[8], (N_HEADS, N_REL), f32)
    w_attn_out = jax.random.normal(ks[9], (D_ATTN, D_MODEL), f32) * D_ATTN ** -0.5 * DEEPNORM_BETA

    w_o = jax.random.normal(ks[10], (D_MODEL, D_MODEL), f32) * D_MODEL ** -0.5 * DEEPNORM_BETA
    b_o = 0.02 * jax.random.normal(ks[11], (D_MODEL,), f32)
    out_ln_g = 1.0 + 0.05 * jax.random.normal(ks[12], (D_MODEL,), f32)
    out_ln_b = 0.02 * jax.random.normal(ks[13], (D_MODEL,), f32)
    return {
        "x": x, "w_in": w_in, "b_in": b_in,
        "conv_w": conv_w, "conv_b": conv_b, "conv_ln_g": conv_ln_g, "conv_ln_b": conv_ln_b,
        "w_conv_out": w_conv_out, "rel_bias": rel_bias, "w_attn_out": w_attn_out,
        "w_o": w_o, "b_o": b_o, "out_ln_g": out_ln_g, "out_ln_b": out_ln_b,
    }


def reference(x, w_in, b_in, conv_w, conv_b, conv_ln_g, conv_ln_b, w_conv_out,
              rel_bias, w_attn_out, w_o, b_o, out_ln_g, out_ln_b):
    offsets = np.cumsum(COL_SPLITS)[:-1].tolist()
    for _ in range(DEPTH):
        z = jnp.einsum("bsd,de->bse", x, w_in) + b_in
        (c_val, c_glu, c_gate, q, k, v, a_gate, g_conv, g_attn) = jnp.split(z, offsets, axis=-1)

        u = c_val * jax.nn.sigmoid(c_glu)
        u = _causal_depthwise_conv(u, conv_w, conv_b)
        u = jax.nn.silu(_layer_norm(u, conv_ln_g, conv_ln_b))
        conv_out = jnp.einsum("bsc,cd->bsd", u * jax.nn.silu(c_gate), w_conv_out)

        o = _chunked_attention(q, k, v, rel_bias)
        attn_out = jnp.einsum("bsc,cd->bsd", o * jax.nn.silu(a_gate), w_attn_out)

        h = jax.nn.sigmoid(g_conv) * conv_out + jax.nn.sigmoid(g_attn) * attn_out
        y = jnp.einsum("bsd,de->bse", h, w_o) + b_o

        x = _layer_norm(DEEPNORM_ALPHA * x + y, out_ln_g, out_ln_b)
    return x
```

```python
import contextlib
import numpy as np
import concourse.bass as bass
import concourse.mybir as mybir
from concourse.bass_utils import run_bass_kernel_spmd

F32 = mybir.dt.float32
BF16 = mybir.dt.bfloat16
AF = mybir.ActivationFunctionType
ALU = mybir.AluOpType

SEQ = 4096
D = 1024
TT = 512
NT = SEQ // TT
DIN = 5632
NG = 11
LN_EPS = 1e-5
ALPHA = 2.0 ** 0.25
CW = 31
HALO = CW - 1

_R = lambda a, b: list(range(a, b))
PERM = (_R(512, 1024)
        + _R(0, 512)
        + _R(1024, 1536)
        + _R(1536, 2048)
        + _R(2048, 2560)
        + _R(2560, 3072)
        + _R(3072, 3584)
        + _R(4608, 5632)
        + _R(3584, 4608))
G_GLU, G_VAL, G_CG, G_Q, G_K, G_V, G_AG, G_GA, G_GC = 0, 1, 2, 3, 4, 5, 6, 7, 9
KPE = 16
NEG = -30000.0


class Sched:
    ENGS = ("pe", "act", "dve", "pool", "sp")

    def __init__(self, nc):
        self.nc = nc
        self.ops = []
        self.last_w = {}
        self.rd_eng = {}
        self.rd_dma = {}
        self.dma_keys = []

    def op(self, eng, fn, reads=(), writes=(), dma=None):
        i = len(self.ops)
        rs, ws = [], []
        for r in reads:
            (ws if r.startswith("ps") else rs).append(r)
        ws += list(writes)
        deps = set()
        for r in rs:
            if r in self.last_w:
                deps.add(self.last_w[r])
        for w in ws:
            if w in self.last_w:
                deps.add(self.last_w[w])
            for j in self.rd_eng.get(w, {}).values():
                deps.add(j)
            for j in self.rd_dma.get(w, ()):
                deps.add(j)
        pr = set()
        for j in deps:
            oj = self.ops[j]
            if oj["dma"] is None and dma is None and oj["eng"] == "pe" and eng == "pe":
                continue
            pr.add(j)
        for j in pr:
            self.ops[j]["signal"] = True
        self.ops.append(dict(eng=eng, fn=fn, deps=pr, dma=dma, signal=False, tok=None))
        for r in rs:
            if dma is None:
                self.rd_eng.setdefault(r, {})[eng] = i
            else:
                self.rd_dma.setdefault(r, []).append(i)
        for w in ws:
            self.last_w[w] = i
            self.rd_eng[w] = {}
            self.rd_dma[w] = []
        if dma is not None and dma not in self.dma_keys:
            self.dma_keys.append(dma)
        return i

    def emit(self, final_wait=()):
        nc = self.nc
        ops = self.ops
        for j in final_wait:
            ops[j]["signal"] = True
        with contextlib.ExitStack() as st:
            sems = {e: st.enter_context(nc.semaphore("s_" + e)) for e in self.ENGS}
            dsems = {k: st.enter_context(nc.semaphore("d_" + k)) for k in self.dma_keys}
            cnt = {e: 0 for e in self.ENGS}
            dcnt = {k: 0 for k in self.dma_keys}
            for o in ops:
                if o["dma"] is not None:
                    dcnt[o["dma"]] += 16
                    o["tok"] = (dsems[o["dma"]], dcnt[o["dma"]])
                    o["signal"] = True
                elif o["signal"]:
                    cnt[o["eng"]] += 1
                    o["tok"] = (sems[o["eng"]], cnt[o["eng"]])
            block = st.enter_context(nc.Block())

            def make(ekey):
                def body(eng):
                    waited = {}
                    for o in ops:
                        if o["eng"] != ekey:
                            continue
                        for j in sorted(o["deps"]):
                            sem, val = ops[j]["tok"]
                            if waited.get(id(sem), 0) >= val:
                                continue
                            eng.wait_ge(sem, val)
                            waited[id(sem)] = val
                        inst = o["fn"](eng)
                        if o["signal"]:
                            inst.then_inc(o["tok"][0], 16 if o["dma"] is not None else 1)
                    if ekey == "sp":
                        for j in final_wait:
                            sem, val = ops[j]["tok"]
                            if waited.get(id(sem), 0) >= val:
                                continue
                            eng.wait_ge(sem, val)
                            waited[id(sem)] = val
                return body

            block.tensor(make("pe"))
            block.scalar(make("act"))
            block.vector(make("dve"))
            block.gpsimd(make("pool"))
            block.sync(make("sp"))


def build_nc(nt_run=NT):
    nc = bass.Bass("TRN2", target_bir_lowering=False)

    def din(name, shape, dt=F32):
        return nc.dram_tensor(name, shape, dt, kind="ExternalInput").ap()

    xT_d = din("xT", [D, SEQ])
    xtok_d = din("xtok", [SEQ, D])
    Wh_d = din("Wh", [NG, 128, 4096])
    wco_d = din("wco", [128, 4, 1024])
    wao_d = din("wao", [128, 4, 1024])
    wo_d = din("wo", [128, 8, 1024])
    bcol_d = din("bcol", [128, 44])
    bvb_d = din("bvb", [1, 512])
    cw_d = din("cw", [128, 4, CW])
    cvec_d = din("cvec", [128, 12])
    BT_d = din("BT", [128, 8, 256])
    cfar_d = din("cfar", [1, 8])
    bo_d = din("bo", [1, D])
    og_d = din("og", [1, D])
    ob_d = din("ob", [1, D])
    out_d = nc.dram_tensor("out", [SEQ, D], F32, kind="ExternalOutput").ap()
    scr_d = nc.dram_tensor("scr", [NG, 128, 4096], BF16, kind="Internal").ap()

    def sb(name, shape, dt=F32):
        return nc.alloc_sbuf_tensor("s_" + name, shape, dt)

    wco = sb("wco16", [128, 4, 1024], BF16)
    wao = sb("wao16", [128, 4, 1024], BF16)
    wo = sb("wo16", [128, 8, 1024], BF16)
    wbuf = [sb(f"wbuf{i}", [128, 8, 512], BF16) for i in range(3)]
    xT16 = [sb(f"xT16_{i}", [128, 8, 512], BF16) for i in range(2)]
    uT = sb("uT", [128, 4, HALO + TT], BF16)
    Dg = sb("Dg", [128, 4, KPE, 128], BF16)
    th = [sb(f"th{i}", [128, TT], F32) for i in range(2)]
    cvb = [sb(f"cvb{i}", [128, TT], F32) for i in range(2)]
    acc = sb("acc", [128, 4, TT], F32)
    y16 = sb("y16", [128, TT], BF16)
    ysq = sb("ysq", [128, TT], BF16)
    sgate = sb("sgate", [128, 4, TT], BF16)
    qm = sb("qm", [128, 4, 4, 2, 128], BF16)
    kT = sb("kT", [128, 4, 1024], BF16)
    vaug = sb("vaug", [128, 8, 520], BF16)
    ag = sb("ag", [128, 4, TT], BF16)
    PT = [sb(f"PT{i}", [128, 1280], BF16) for i in range(2)]
    on = [sb(f"on{i}", [128, 512], BF16) for i in range(2)]
    rden = sb("rden", [128, 4], F32)
    goT = sb("goT", [128, 4, TT], BF16)
    thg = [sb(f"thg{i}", [128, TT], F32) for i in range(2)]
    t1 = [sb(f"t1_{i}", [128, TT], F32) for i in range(2)]
    hT = sb("hT", [128, 8, TT], BF16)
    rbuf = [sb(f"rbuf{i}", [128, D], F32) for i in range(4)]
    g_bc = sb("g_bc", [128, D], F32)
    b_bc = sb("b_bc", [128, D], F32)
    bvb_hl = sb("bvb_hl", [128, 512], BF16)
    Bsc = sb("Bsc", [128, 4, 2, 2, 128], BF16)
    mean_sb = sb("mean_sb", [128, TT], F32)
    bcol = sb("bcol", [128, 44], F32)
    hb = sb("hb", [128, 44], F32)
    cw2 = sb("cw2", [128, 4, CW], F32)
    cvec = sb("cvec", [128, 12], F32)
    cfar = sb("cfar", [128, 8], F32)
    maskT = sb("maskT", [128, 2, 128], BF16)
    epst = sb("epst", [128, 1], F32)
    mhalf = sb("mhalf", [128, 1], F32)
    ident = sb("ident", [128, 128], BF16)
    identf = sb("identf", [128, 128], F32)
    onesc = sb("onesc", [128, 128], BF16)
    bo_hl = sb("bo_hl", [128, D], BF16)
    ones33 = sb("ones33", [128, 128], BF16)
    st = sb("st", [128, 32], F32)
    bst = sb("bst", [128, 4, 12], F32)

    junk = mean_sb[:, :].bitcast(BF16)
    b33, d33, hi33 = rbuf[0][0:33, :], rbuf[1][0:33, :], rbuf[2][0:33, :].bitcast(BF16)[:, 0:D]
    var_sb, rstd_sb = t1[0], t1[1]

    scs = [nc.alloc_psum_tensor(f"sc{i}", [128, 1536], F32) for i in range(2)]
    _b6 = nc.alloc_psum_tensor("bank6", [128, 512], F32)
    _b7 = nc.alloc_psum_tensor("bank7", [128, 512], F32)
    banks = [scs[0][:, 0:512], scs[0][:, 512:1024], scs[0][:, 1024:1536],
             scs[1][:, 0:512], scs[1][:, 512:1024], scs[1][:, 1024:1536], _b6[:, :], _b7[:, :]]

    S = Sched(nc)
    mult, add, sub = ALU.mult, ALU.add, ALU.subtract

    def xT_load(T, extra_reads=()):
        t0 = T * TT
        src = xT_d.rearrange("(kc p) t -> p kc t", p=128)[:, :, t0:t0 + TT]
        S.op("pool", lambda e, T=T, src=src: e.dma_start(out=xT16[T % 2][:, :, :], in_=src),
             reads=list(extra_reads), writes=[f"xT{T % 2}"], dma=f"xT{T % 2}")

    def scr_cast(g):
        src = Wh_d[g].rearrange("p (a b) -> p a b", b=2048)
        dst = scr_d[g].rearrange("p (a b) -> p a b", b=2048)
        S.op("pool", lambda e, src=src, dst=dst: e.dma_start(out=dst, in_=src),
             reads=[], writes=[f"scr{g}"], dma=f"scr{g}")

    xT_load(0)

    def sp_load(dst, src, key):
        S.op("sp", lambda e: e.dma_start(out=dst, in_=src), reads=[], writes=[key], dma="ld_" + key)

    sp_load(bcol[:, :], bcol_d[:, :], "bcol")
    sp_load(cw2[:, :, :], cw_d[:, :, :], "cw2")
    sp_load(cvec[:, :], cvec_d[:, :], "cvec")
    sp_load(cfar[:, :], cfar_d[0:1, :].broadcast_to([128, 8]), "cfar")
    BTf = acc[:, :, :].rearrange("p a (b c) -> p (a b) c", c=256)
    S.op("sp", lambda e: e.dma_start(out=BTf, in_=BT_d[:, :, :]), reads=[], writes=[f"acc{k}" for k in range(4)], dma="ld_BT")
    sp_load(g_bc[:, :], og_d[0:1, :].broadcast_to([128, D]), "g_bc")
    sp_load(b_bc[:, :], ob_d[0:1, :].broadcast_to([128, D]), "b_bc")
    S.op("pool", lambda e: e.memset(identf[:, :], 0.0), writes=["identf"])
    S.op("pool", lambda e: e.affine_select(out=identf[:, :], in_=identf[:, :], pattern=[[-1, 128]],
                                           compare_op=ALU.not_equal, fill=1.0, base=0, channel_multiplier=1),
         reads=["identf"], writes=["identf"])
    S.op("pool", lambda e: e.memset(b33[:, :], 0.0), writes=["rbuf0"])
    S.op("sp", lambda e: e.dma_start(out=b33[0:1, :], in_=bo_d[0:1, :]), reads=[], writes=["rbuf0"], dma="ld_b33a")
    S.op("sp", lambda e: e.dma_start(out=b33[32:33, :], in_=bo_d[0:1, :]), reads=[], writes=["rbuf0"], dma="ld_b33b")

    S.op("pool", lambda e: e.memset(uT[:, :, :], 0.0), writes=[f"uT{k}" for k in range(4)])
    S.op("pool", lambda e: e.memset(vaug[:, :, :], 1.0), writes=[f"va{s}" for s in range(8)])
    S.op("pool", lambda e: e.memset(epst[:, :], LN_EPS), writes=["epst"])
    S.op("pool", lambda e: e.memset(mhalf[:, :], -0.5), writes=["mhalf"])
    S.op("pool", lambda e: e.memset(onesc[:, :], 1.0 / 512.0), writes=["onesc"])
    S.op("pool", lambda e: e.memset(ones33[:, :], 1.0), writes=["ones33"])
    S.op("dve", lambda e: e.tensor_copy(out=ident[:, :], in_=identf[:, :]), reads=["identf"], writes=["ident"])
    S.op("dve", lambda e: e.tensor_scalar(out=hb[:, :], in0=bcol[:, :], scalar1=0.5, scalar2=None, op0=mult),
         reads=["bcol"], writes=["hb"])
    S.op("dve", lambda e: e.tensor_scalar(out=cw2[:, :, :], in0=cw2[:, :, :], scalar1=0.5, scalar2=None, op0=mult),
         reads=["cw2"], writes=["cw2"])
    def build_dg(kc):
        for j in range(KPE):
            S.op("act", lambda e, kc=kc, j=j: e.activation(out=Dg[:, kc, j, :], in_=identf[:, :], func=AF.Copy,
                                                           scale=cw2[:, kc, j:j + 1]),
                 reads=["identf", "cw2"], writes=[f"Dg{kc}"])

    build_dg(0)
    S.op("dve", lambda e: e.memset(maskT[:, :, :], 0.0), writes=["maskT"])
    S.op("dve", lambda e: e.memset(maskT[0:64, :, 64:128], NEG), reads=["maskT"], writes=["maskT"])
    for h_ in range(8):
        for jj in range(2):
            S.op("dve", lambda e, h_=h_, jj=jj: e.tensor_scalar(
                out=Bsc[:, h_ // 2, jj, h_ % 2, :], in0=BTf[:, h_, jj * 128:(jj + 1) * 128],
                scalar1=cfar[:, h_:h_ + 1], scalar2=8.0, op0=sub, op1=mult),
                reads=[f"acc{k}" for k in range(4)] + ["cfar"], writes=["Bsc"])
    for hc_ in range(4):
        S.op("dve", lambda e, hc_=hc_: e.memset(Bsc[64:128, hc_, 1, :, 0:64], NEG), reads=["Bsc"], writes=["Bsc"])
    S.op("pool", lambda e: e.memset(qm[:, :, :, :, :].rearrange("p a b c d -> p (a b c d)"), 0.0), writes=[f"qm{k}" for k in range(4)])
    S.op("dve", lambda e: e.tensor_copy(out=hi33, in_=b33[:, :]), reads=["rbuf0"], writes=["rbuf2"])
    S.op("dve", lambda e: e.tensor_tensor(out=d33[:, :], in0=b33[:, :], in1=hi33, op=sub),
         reads=["rbuf0", "rbuf2"], writes=["rbuf1"])
    S.op("dve", lambda e: e.memset(bo_hl[:, :], 0.0), writes=["bo_hl"])
    S.op("dve", lambda e: e.tensor_copy(out=bo_hl[0:33, :], in_=hi33), reads=["rbuf2", "bo_hl"], writes=["bo_hl"])
    S.op("dve", lambda e: e.tensor_copy(out=bo_hl[32:33, :], in_=d33[32:33, :]), reads=["rbuf1", "bo_hl"], writes=["bo_hl"])
    bv_st, bv_df, bv_hi = rbuf[3][0:33, 0:512], rbuf[3][0:33, 512:1024], junk[0:33, 0:512]
    S.op("pool", lambda e: e.memset(rbuf[3][0:33, :], 0.0), writes=["rbuf3"])
    S.op("sp", lambda e: e.dma_start(out=rbuf[3][0:1, 0:512], in_=bvb_d[0:1, :]), reads=[], writes=["rbuf3"], dma="ld_bv_a")
    S.op("sp", lambda e: e.dma_start(out=rbuf[3][32:33, 0:512], in_=bvb_d[0:1, :]), reads=[], writes=["rbuf3"], dma="ld_bv_b")
    S.op("dve", lambda e: e.tensor_copy(out=bv_hi, in_=bv_st), reads=["rbuf3"], writes=["mean_sb"])
    S.op("dve", lambda e: e.tensor_tensor(out=bv_df, in0=bv_st, in1=bv_hi, op=sub), reads=["rbuf3", "mean_sb"], writes=["rbuf3"])
    S.op("dve", lambda e: e.memset(bvb_hl[:, :], 0.0), writes=["bvb_hl"])
    S.op("dve", lambda e: e.tensor_copy(out=bvb_hl[0:33, :], in_=bv_hi), reads=["mean_sb", "bvb_hl"], writes=["bvb_hl"])
    S.op("dve", lambda e: e.tensor_copy(out=bvb_hl[32:33, :], in_=rbuf[3][32:33, 512:1024]), reads=["rbuf3", "bvb_hl"], writes=["bvb_hl"])

    zb = [0]

    def next_bank(choices=(0, 1, 6, 7)):
        b = choices[zb[0] % len(choices)]
        zb[0] += 1
        return b

    wl_next = [0]
    n_groups_total = nt_run * NG

    def wload_upto(n_max):
        while wl_next[0] < min(n_max, n_groups_total):
            n = wl_next[0]
            g = n % NG
            slot = n % 3
            if n < NG:
                dst = wbuf[slot][:, :, :].rearrange("p k c -> p (k c)").rearrange("p (a b) -> p a b", b=2048)
                src = Wh_d[g].rearrange("p (a b) -> p a b", b=2048)
                S.op("pool", lambda e, dst=dst, src=src: e.dma_start(out=dst, in_=src),
                     reads=[], writes=[f"wbuf{slot}"], dma=f"wbq{slot}")
            else:
                S.op("sp", lambda e, g=g, slot=slot: e.dma_start(out=wbuf[slot][:, :, :],
                                                                 in_=scr_d[g].rearrange("p (k c) -> p k c", c=512)),
                     reads=[f"scr{g}"], writes=[f"wbuf{slot}"], dma=f"wb{slot}")
            wl_next[0] += 1

    def zchunk(T, g, ec, bank):
        n = T * NG + g
        slot = n % 3
        xb = T % 2

        def fn(e):
            for kc in range(8):
                last = e.matmul(banks[bank][:, :], lhsT=wbuf[slot][:, kc, ec * 128:(ec + 1) * 128],
                                rhs=xT16[xb][:, kc, :], start=(kc == 0), stop=(kc == 7))
            return last
        S.op("pe", fn, reads=[f"wbuf{slot}", f"xT{xb}"], writes=[f"ps{bank}"])

    def group_done(T, g):
        wload_upto(T * NG + g + 4)
        if T == 0 and nt_run > 1:
            scr_pending.append(g)

    deferred = []
    scr_pending = []

    def scr_flush(limit=None):
        while scr_pending and (limit is None or scr_pending[0] < limit):
            scr_cast(scr_pending.pop(0))
    tail_mode = [False]

    def run_deferred(n):
        while n > 0 and deferred:
            deferred.pop(0)()
            n -= 1

    out_ops = []
    wload_upto(3)
    for T in range(nt_run):
        t0 = T * TT
        xb = T % 2

        conv_nj = [KPE] * 4
        conv_ready = [False] * 4
        conv_last = [3]

        def conv_left():
            return sum(CW - x for x in conv_nj)

        def conv_pump(n):
            while n > 0:
                pick = None
                for d_ in range(1, 5):
                    kc = (conv_last[0] + d_) % 4
                    if conv_ready[kc] and conv_nj[kc] < CW:
                        pick = kc
                        break
                if pick is None:
                    return
                kc = pick
                j = conv_nj[kc]
                conv_nj[kc] += 1
                conv_last[0] = kc
                n -= 1
                if j == KPE:
                    cb = 2 + kc
                    S.op("dve", lambda e, kc=kc, j=j, cb=cb: e.scalar_tensor_tensor(
                        out=acc[:, kc, :], in0=uT[:, kc, j:j + TT], scalar=cw2[:, kc, j:j + 1], in1=banks[cb][:, :],
                        op0=mult, op1=add), reads=[f"uT{kc}", "cw2", f"ps{cb}"], writes=[f"acc{kc}"])
                else:
                    S.op("dve", lambda e, kc=kc, j=j: e.scalar_tensor_tensor(
                        out=acc[:, kc, :], in0=uT[:, kc, j:j + TT], scalar=cw2[:, kc, j:j + 1], in1=acc[:, kc, :],
                        op0=mult, op1=add), reads=[f"uT{kc}", "cw2", f"acc{kc}"], writes=[f"acc{kc}"])

        def pe_taps(kc):
            cb = 2 + kc

            def fconv(e):
                for j in range(KPE):
                    last = e.matmul(banks[cb][:, :], lhsT=Dg[:, kc, j, :], rhs=uT[:, kc, j:j + TT],
                                    start=(j == 0), stop=(j == KPE - 1))
                return last
            S.op("pe", fconv, reads=[f"Dg{kc}", f"uT{kc}"], writes=[f"ps{cb}"])
            conv_ready[kc] = True
            conv_pump(2)

        for ec in range(4):
            bA = next_bank()
            zchunk(T, G_GLU, ec, bA)
            jb = G_GLU * 4 + ec
            S.op("act", lambda e, bA=bA, jb=jb, ec=ec: e.activation(
                out=th[ec % 2][:, :], in_=banks[bA][:, :], func=AF.Tanh, bias=hb[:, jb:jb + 1], scale=0.5),
                reads=[f"ps{bA}", "hb"], writes=[f"th{ec % 2}"])
            bB = next_bank()
            zchunk(T, G_VAL, ec, bB)
            jv = G_VAL * 4 + ec
            S.op("act", lambda e, bB=bB, jv=jv, ec=ec: e.activation(
                out=cvb[ec % 2][:, :], in_=banks[bB][:, :], func=AF.Identity, bias=bcol[:, jv:jv + 1], scale=1.0),
                reads=[f"ps{bB}", "bcol"], writes=[f"cvb{ec % 2}"])
            S.op("dve", lambda e, ec=ec: e.scalar_tensor_tensor(
                out=uT[:, ec, HALO:HALO + TT], in0=th[ec % 2][:, :], scalar=1.0, in1=cvb[ec % 2][:, :],
                op0=add, op1=mult), reads=[f"th{ec % 2}", f"cvb{ec % 2}"], writes=[f"uT{ec}"])
            if T == 0 and ec < 3:
                build_dg(ec + 1)
            if ec >= 2:
                pe_taps(ec - 2)
            run_deferred(2)
        group_done(T, G_GLU)
        group_done(T, G_VAL)
        for ec in range(4):
            bk = next_bank()
            zchunk(T, G_CG, ec, bk)
            jb = G_CG * 4 + ec
            S.op("act", lambda e, bk=bk, jb=jb, ec=ec: e.activation(
                out=sgate[:, ec, :], in_=banks[bk][:, :], func=AF.Silu, bias=bcol[:, jb:jb + 1], scale=1.0),
                reads=[f"ps{bk}", "bcol"], writes=[f"sgate{ec}"])
            if ec == 0:
                pe_taps(2)
            elif ec == 1:
                pe_taps(3)
            else:
                conv_pump(3)
            run_deferred(2)
        group_done(T, G_CG)
        for hc in range(4):
            bk = next_bank((0, 1, 6, 7, 2, 3, 4, 5))
            zchunk(T, G_Q, hc, bk)
            jb = G_Q * 4 + hc
            for hh_ in range(2):
                pl = slice(hh_ * 64, hh_ * 64 + 64)
                S.op("act", lambda e, bk=bk, jb=jb, hc=hc, hh_=hh_, pl=pl: e.activation(
                    out=qm[pl, hc, :, hh_, :], in_=banks[bk][pl, :].rearrange("p (b q) -> p b q", q=128),
                    func=AF.Identity, bias=bcol[pl, jb:jb + 1], scale=1.0),
                    reads=[f"ps{bk}", "bcol"], writes=[f"qm{hc}"])
            conv_pump(2)
            run_deferred(2)
        group_done(T, G_Q)
        rc = (t0 % 1024)
        for hc in range(4):
            bk = next_bank((0, 1, 6, 7, 2, 3, 4, 5))
            zchunk(T, G_K, hc, bk)
            jb = G_K * 4 + hc
            S.op("act", lambda e, bk=bk, jb=jb, hc=hc, rc=rc: e.activation(
                out=kT[:, hc, rc:rc + TT], in_=banks[bk][:, :], func=AF.Identity, bias=bcol[:, jb:jb + 1], scale=1.0),
                reads=[f"ps{bk}", "bcol"], writes=[f"kT{hc}_{(4 * T + i) % 8}" for i in range(4)])
            conv_pump(2)
            run_deferred(2)
        group_done(T, G_K)
        n_v = T * NG + G_V
        slot_v = n_v % 3
        for tb in range(4):
            bk = next_bank((0, 1, 6, 7, 2, 3, 4, 5))

            def fnv(e, tb=tb, bk=bk, xb=xb, slot_v=slot_v):
                for kc in range(8):
                    e.matmul(banks[bk][:, :], lhsT=xT16[xb][:, kc, tb * 128:(tb + 1) * 128],
                             rhs=wbuf[slot_v][:, kc, :], start=(kc == 0), stop=False)
                return e.matmul(banks[bk][:, :], lhsT=ones33[:, :], rhs=bvb_hl[:, :], start=False, stop=True)
            S.op("pe", fnv, reads=[f"wbuf{slot_v}", f"xT{xb}", "ones33", "bvb_hl"], writes=[f"ps{bk}"])
            vs = (4 * T + tb) % 8
            S.op("act", lambda e, bk=bk, vs=vs: e.activation(
                out=vaug[:, vs, :].rearrange("p (h d) -> p h d", d=65)[:, :, 0:64],
                in_=banks[bk][:, :].rearrange("p (h d) -> p h d", d=64), func=AF.Copy),
                reads=[f"ps{bk}"], writes=[f"va{vs}"])
            conv_pump(2)
            run_deferred(2)
        group_done(T, G_V)
        for hc in range(4):
            bk = next_bank((0, 1, 6, 7, 2, 3, 4, 5))
            zchunk(T, G_AG, hc, bk)
            jb = G_AG * 4 + hc
            S.op("act", lambda e, bk=bk, jb=jb, hc=hc: e.activation(
                out=ag[:, hc, :], in_=banks[bk][:, :], func=AF.Silu, bias=bcol[:, jb:jb + 1], scale=1.0),
                reads=[f"ps{bk}", "bcol"], writes=[f"ag{hc}"])
            conv_pump(2)
            run_deferred(2)
        group_done(T, G_AG)
        run_deferred(10 ** 6)
        for tb in range(4):
            row0 = t0 + tb * 128
            S.op("sp", lambda e, tb=tb, row0=row0: e.dma_start(out=rbuf[tb][:, :], in_=xtok_d[row0:row0 + 128, :]),
                 reads=[], writes=[f"rbuf{tb}"], dma=f"xr{tb}")

        if T == 0:
            S.op("pool", lambda e: e.dma_start(out=wao[:, :, :], in_=wao_d[:, :, :]), reads=[], writes=["wao"], dma="ld_wao")
            S.op("pool", lambda e: e.dma_start(out=wco[:, :, :], in_=wco_d[:, :, :]), reads=[], writes=["wco"], dma="ld_wco")
            S.op("pool", lambda e: e.dma_start(out=wo[:, :, :], in_=wo_d[:, :, :]), reads=[], writes=["wo"], dma="ld_wo")
        if T + 1 < nt_run:
            xT_load(T + 1)
        scr_flush(3)
        pairs = [(bl, hc) for bl in range(4) for hc in range(4)]
        SCOL = lambda j: 256 + 256 * j

        def emit_qk(p):
            bl, hc = pairs[p]
            b = 4 * T + bl
            st_ = p % 2
            sc = scs[st_]
            jmin = max(0, 4 - b)
            q2 = qm[:, hc, bl, :, :].rearrange("p a b -> p (a b)")

            def fn(e):
                for j in range(jmin, 5):
                    kt = b - 4 + j
                    col = (kt % 8) * 128
                    o = sc[:, SCOL(j):SCOL(j) + 256]
                    last = e.matmul(o, lhsT=kT[:, hc, col:col + 128], rhs=q2, start=True, stop=(j in (1, 2)))
                    if j >= 3:
                        last = e.matmul(o, lhsT=ident[:, :], rhs=Bsc[:, hc, j - 3, :, :].rearrange("p a b -> p (a b)"),
                                        start=False, stop=True)
                    if j == 0:
                        last = e.matmul(o, lhsT=ident[:, :], rhs=maskT[:, :, :].rearrange("p a b -> p (a b)"),
                                        start=False, stop=True)
                return last
            rds = [f"qm{hc}", "ident", "Bsc", "maskT"] + [f"kT{hc}_{(b - 4 + j) % 8}" for j in range(jmin, 5)]
            S.op("pe", fn, reads=rds, writes=[f"ps{3 * st_ + i}" for i in range(3)])

        def emit_softmax(p):
            bl, hc = pairs[p]
            b = 4 * T + bl
            st_ = p % 2
            jmin = max(0, 4 - b)
            c0 = SCOL(jmin)
            S.op("act", lambda e: e.activation(out=PT[st_][:, c0 - 256:1280], in_=scs[st_][:, c0:1536], func=AF.Exp,
                                               scale=0.125),
                 reads=[f"ps{3 * st_ + i}" for i in range(3)], writes=[f"PT{st_}"])

        def emit_pv(p):
            bl, hc = pairs[p]
            b = 4 * T + bl
            st_ = p % 2
            P = PT[st_]
            sl = lambda j: (b - 4 + j) % 8
            js = [j for j in range(5) if b - 4 + j >= 0]
            half = hc // 2
            ob = banks[6 + half]
            mms = []
            for hh in range(2):
                h = 2 * hc + hh
                c = (h % 4) * 65
                for i, j in enumerate(js):
                    mms.append((ob[:, c:c + 65], P[:, j * 256 + hh * 128:j * 256 + hh * 128 + 128],
                                vaug[:, sl(j), h * 65:(h + 1) * 65], i == 0, i == len(js) - 1))

            def fn(e):
                for (o, l, r, st0, sp0) in mms:
                    last = e.matmul(o, lhsT=l, rhs=r, start=st0, stop=sp0)
                return last
            S.op("pe", fn, reads=[f"PT{st_}"] + [f"va{sl(j)}" for j in js], writes=[f"ps{6 + half}"])

        def emit_norm(bl, half):
            ov = banks[6 + half][:, 0:260].rearrange("p (h d) -> p h d", d=65)
            os_ = bl % 2
            S.op("dve", lambda e: e.reciprocal(out=rden[:, :].unsqueeze(2), in_=ov[:, :, 64:65]),
                 reads=[f"ps{6 + half}"], writes=["rden"])
            S.op("dve", lambda e: e.tensor_tensor(
                out=on[os_][:, half * 256:(half + 1) * 256].rearrange("p (h d) -> p h d", d=64),
                in0=ov[:, :, 0:64], in1=rden[:, :].unsqueeze(2).broadcast_to([128, 4, 64]), op=mult),
                reads=[f"ps{6 + half}", "rden"], writes=[f"on{os_}_{half}"])

        def emit_transpose(bl, tbk):
            os_ = bl % 2
            b16 = banks[tbk][:, 0:256].bitcast(BF16)

            def fn(e):
                for hc in range(4):
                    last = e.transpose(out=b16[:, hc * 128:(hc + 1) * 128], in_=on[os_][:, hc * 128:(hc + 1) * 128],
                                       identity=ident[:, :])
                return last
            S.op("pe", fn, reads=[f"on{os_}_0", f"on{os_}_1", "ident"], writes=[f"ps{tbk}"])
            S.op("dve", lambda e: e.tensor_tensor(
                out=goT[:, :, bl * 128:(bl + 1) * 128], in0=b16[:, 0:512].rearrange("p (c q) -> p c q", q=128),
                in1=ag[:, :, bl * 128:(bl + 1) * 128], op=mult),
                reads=[f"ps{tbk}"] + [f"ag{i}" for i in range(4)], writes=["goT"])

        npairs = len(pairs)
        emit_qk(0)
        emit_softmax(0)
        pend_tr = None
        for p in range(npairs):
            if p + 1 < npairs:
                emit_qk(p + 1)
                emit_softmax(p + 1)
            emit_pv(p)
            bl, hc = pairs[p]
            if hc == 1:
                emit_norm(bl, 0)
            if hc == 3:
                emit_norm(bl, 1)
                pend_tr = bl
            if pend_tr is not None and (hc == 2 or p == npairs - 1):
                emit_transpose(pend_tr, 3 * (p % 2))
                pend_tr = None
            conv_pump(2)

        conv_pump(10 ** 6)
        for kc in range(4):
            S.op("dve", lambda e, kc=kc: e.tensor_copy(out=uT[:, kc, 0:HALO], in_=uT[:, kc, TT:TT + HALO]),
                 reads=[f"uT{kc}"], writes=[f"uT{kc}"])
        def ln_stats(kc):
            S.op("act", lambda e, kc=kc: e.activation(out=y16[:, :], in_=acc[:, kc, :], func=AF.Identity,
                                                      bias=cvec[:, kc:kc + 1], scale=1.0),
                 reads=[f"acc{kc}", "cvec"], writes=["y16"])
            S.op("act", lambda e, kc=kc: e.activation(out=ysq[:, :], in_=acc[:, kc, :], func=AF.Square,
                                                      bias=cvec[:, kc:kc + 1], scale=1.0),
                 reads=[f"acc{kc}", "cvec"], writes=["ysq"])
            S.op("pe", lambda e, kc=kc: e.matmul(banks[4][:, :], lhsT=onesc[:, :], rhs=y16[:, :],
                                                 start=(kc == 0), stop=(kc == 3)),
                 reads=["onesc", "y16"], writes=["ps4"])
            S.op("pe", lambda e, kc=kc: e.matmul(banks[5][:, :], lhsT=onesc[:, :], rhs=ysq[:, :],
                                                 start=(kc == 0), stop=(kc == 3)),
                 reads=["onesc", "ysq"], writes=["ps5"])

        def ln_rstd_a():
            S.op("act", lambda e: e.activation(out=mean_sb[:, :], in_=banks[4][:, :], func=AF.Copy),
                 reads=["ps4"], writes=["mean_sb"])
            S.op("dve", lambda e: e.scalar_tensor_tensor(out=var_sb[:, :], in0=mean_sb[:, :], scalar=-1.0, in1=mean_sb[:, :],
                                                         op0=mult, op1=mult), reads=["mean_sb"], writes=["t1_0"])
            S.op("dve", lambda e: e.tensor_tensor(out=var_sb[:, :], in0=banks[5][:, :], in1=var_sb[:, :], op=add),
                 reads=["ps5", "t1_0"], writes=["t1_0"])

        def ln_rstd_b():
            S.op("act", lambda e: e.activation(out=rstd_sb[:, :], in_=var_sb[:, :], func=AF.Sqrt, bias=epst[:, 0:1], scale=1.0),
                 reads=["t1_0", "epst"], writes=["t1_1"])
            S.op("dve", lambda e: e.reciprocal(out=rstd_sb[:, :], in_=rstd_sb[:, :]), reads=["t1_1"], writes=["t1_1"])

        def ln_apply(kc):
            S.op("dve", lambda e, kc=kc: e.scalar_tensor_tensor(out=acc[:, kc, :], in0=acc[:, kc, :],
                                                                scalar=cvec[:, kc:kc + 1], in1=mean_sb[:, :],
                                                                op0=add, op1=sub),
                 reads=[f"acc{kc}", "mean_sb", "cvec"], writes=[f"acc{kc}"])
            S.op("dve", lambda e, kc=kc: e.tensor_tensor(out=acc[:, kc, :], in0=acc[:, kc, :], in1=rstd_sb[:, :], op=mult),
                 reads=[f"acc{kc}", "t1_1"], writes=[f"acc{kc}"])

        def ln_silu(kc):
            S.op("act", lambda e, kc=kc: e.activation(out=acc[:, kc, :], in_=acc[:, kc, :], func=AF.Silu,
                                                      scale=cvec[:, 4 + kc:5 + kc], bias=cvec[:, 8 + kc:9 + kc]),
                 reads=[f"acc{kc}", "cvec"], writes=[f"acc{kc}"])
            S.op("pool", lambda e, kc=kc: e.tensor_tensor(out=sgate[:, kc, :], in0=acc[:, kc, :], in1=sgate[:, kc, :], op=mult),
                 reads=[f"acc{kc}", f"sgate{kc}"], writes=[f"sgate{kc}"])

        ln_steps = [lambda: (ln_stats(0), ln_stats(1)), lambda: (ln_stats(2), ln_stats(3)), ln_rstd_a, ln_rstd_b,
                    lambda: (ln_apply(0), ln_apply(1)), lambda: (ln_apply(2), ln_apply(3), ln_silu(0), ln_silu(1)),
                    lambda: (ln_silu(2), ln_silu(3))]

        if T == 0:
            S.op("dve", lambda e: e.tensor_scalar(out=wo[:, :, :], in0=wo[:, :, :], scalar1=0.5, scalar2=None, op0=mult),
                 reads=["wo"], writes=["wo"])
        lb = [0]

        def late_bank():
            b_ = (0, 1, 2, 3)[lb[0] % 4]
            lb[0] += 1
            return b_

        for dc in range(8):
            g_a = G_GA + dc // 4
            ec = dc % 4
            bk = late_bank()
            zchunk(T, g_a, ec, bk)
            ja = g_a * 4 + ec
            ta = thg[dc % 2]
            S.op("act", lambda e, bk=bk, ja=ja, ta=ta: e.activation(out=ta[:, :], in_=banks[bk][:, :], func=AF.Tanh,
                                                                    bias=hb[:, ja:ja + 1], scale=0.5),
                 reads=[f"ps{bk}", "hb"], writes=[f"thg{dc % 2}"])
            bk2 = late_bank()

            def fao(e, bk2=bk2, dc=dc):
                for kc in range(4):
                    last = e.matmul(banks[bk2][:, :], lhsT=wao[:, kc, dc * 128:(dc + 1) * 128], rhs=goT[:, kc, :],
                                    start=(kc == 0), stop=(kc == 3))
                return last
            S.op("pe", fao, reads=["wao", "goT"], writes=[f"ps{bk2}"])
            S.op("dve", lambda e, bk2=bk2, ta=ta, dc=dc: e.scalar_tensor_tensor(
                out=hT[:, dc, :], in0=ta[:, :], scalar=1.0, in1=banks[bk2][:, :], op0=add, op1=mult),
                reads=[f"ps{bk2}", f"thg{dc % 2}"], writes=[f"hT{dc}"])
            if ln_steps:
                ln_steps.pop(0)()
            if ec == 3:
                group_done(T, g_a)

        while ln_steps:
            ln_steps.pop(0)()
        scr_flush()
        gate_bank = {}

        def gate_part(dc):
            g_c = G_GC + dc // 4
            ec = dc % 4
            bk3 = late_bank()
            gate_bank[dc] = bk3
            zchunk(T, g_c, ec, bk3)
            jc = g_c * 4 + ec
            tc_ = thg[dc % 2]
            S.op("act", lambda e: e.activation(out=tc_[:, :], in_=banks[bk3][:, :], func=AF.Tanh,
                                               bias=hb[:, jc:jc + 1], scale=0.5),
                 reads=[f"ps{bk3}", "hb"], writes=[f"thg{dc % 2}"])
            if ec == 3:
                group_done(T, g_c)

        def co_part(dc):
            tc_ = thg[dc % 2]
            bk4 = late_bank()

            def fco(e):
                for kc in range(4):
                    last = e.matmul(banks[bk4][:, :], lhsT=wco[:, kc, dc * 128:(dc + 1) * 128], rhs=sgate[:, kc, :],
                                    start=(kc == 0), stop=(kc == 3))
                return last
            S.op("pe", fco, reads=["wco"] + [f"sgate{k}" for k in range(4)], writes=[f"ps{bk4}"])
            tt_ = t1[dc % 2]
            S.op("dve", lambda e: e.scalar_tensor_tensor(
                out=tt_[:, :], in0=tc_[:, :], scalar=1.0, in1=banks[bk4][:, :], op0=add, op1=mult),
                reads=[f"ps{bk4}", f"thg{dc % 2}"], writes=[f"t1_{dc % 2}"])
            S.op("pool", lambda e: e.tensor_tensor(out=hT[:, dc, :], in0=tt_[:, :], in1=hT[:, dc, :], op=add),
                 reads=[f"t1_{dc % 2}", f"hT{dc}"], writes=[f"hT{dc}"])

        gate_part(0)
        for dc in range(8):
            if dc + 1 < 8:
                gate_part(dc + 1)
            co_part(dc)

        scr_flush()
        stg_a, stg_b, stg_c = [], [], []
        for tb in range(4):
            r = rbuf[tb]
            row0 = t0 + tb * 128
            yb = (4, 6)[tb % 2]
            for hf in range(2):
                def fy1(e, tb=tb, hf=hf, yb=yb):
                    for kc in range(6):
                        last = e.matmul(banks[yb + hf][:, :], lhsT=hT[:, kc, tb * 128:(tb + 1) * 128],
                                        rhs=wo[:, kc, hf * 512:(hf + 1) * 512], start=(kc == 0), stop=False)
                    return last

                def fy2(e, tb=tb, hf=hf, yb=yb):
                    for kc in range(6, 8):
                        e.matmul(banks[yb + hf][:, :], lhsT=hT[:, kc, tb * 128:(tb + 1) * 128],
                                 rhs=wo[:, kc, hf * 512:(hf + 1) * 512], start=False, stop=False)
                    return e.matmul(banks[yb + hf][:, :], lhsT=ones33[:, :], rhs=bo_hl[:, hf * 512:(hf + 1) * 512],
                                    start=False, stop=True)
                S.op("pe", fy1, reads=["wo"] + [f"hT{k}" for k in range(6)], writes=[f"ps{yb + hf}"])
                S.op("pe", fy2, reads=["wo", "ones33", "bo_hl", "hT6", "hT7"], writes=[f"ps{yb + hf}"])
                S.op("dve", lambda e, r=r, hf=hf, yb=yb: e.scalar_tensor_tensor(
                    out=r[:, hf * 512:(hf + 1) * 512], in0=r[:, hf * 512:(hf + 1) * 512], scalar=ALPHA,
                    in1=banks[yb + hf][:, :], op0=mult, op1=add),
                    reads=[f"ps{yb + hf}", f"rbuf{tb}"], writes=[f"rbuf{tb}"])
                S.op("dve", lambda e, r=r, hf=hf, tb=tb: e.bn_stats(out=bst[:, tb, hf * 6:(hf + 1) * 6],
                                                                    in_=r[:, hf * 512:(hf + 1) * 512]),
                     reads=[f"rbuf{tb}"], writes=[f"bst{tb}_{hf}"])
            so_ = tb * 8
            S.op("dve", lambda e, tb=tb, so_=so_: e.bn_aggr(out=st[:, so_ + 2:so_ + 4], in_=bst[:, tb, :]),
                 reads=[f"bst{tb}_0", f"bst{tb}_1"], writes=[f"st{tb}c"])

            def stage_a(tb=tb, r=r):
                return

            def stage_b(tb=tb, r=r):
                so = tb * 8
                S.op("pool", lambda e: e.tensor_scalar(out=st[:, so + 4:so + 5], in0=st[:, so + 3:so + 4], scalar1=1.0,
                                                       scalar2=LN_EPS, op0=mult, op1=add),
                     reads=[f"st{tb}c"], writes=[f"st{tb}e"])
                S.op("pool", lambda e: e.tensor_tensor(out=st[:, so + 5:so + 6], in0=st[:, so + 4:so + 5],
                                                       in1=mhalf[:, 0:1], op=ALU.pow),
                     reads=[f"st{tb}e", "mhalf"], writes=[f"st{tb}f"])
                S.op("dve", lambda e: e.tensor_scalar(out=st[:, so + 6:so + 7], in0=st[:, so + 2:so + 3],
                                                      scalar1=st[:, so + 5:so + 6], scalar2=-1.0, op0=mult, op1=mult),
                     reads=[f"st{tb}c", f"st{tb}f"], writes=[f"st{tb}g"])

            def stage_c(tb=tb, r=r, row0=row0):
                so = tb * 8
                S.op("act", lambda e: e.activation(out=r[:, :], in_=r[:, :], func=AF.Identity,
                                                   scale=st[:, so + 5:so + 6], bias=st[:, so + 6:so + 7]),
                     reads=[f"rbuf{tb}", f"st{tb}f", f"st{tb}g"], writes=[f"rbuf{tb}"])
                aff = "dve" if (tail_mode[0] and tb >= 1) else "pool"
                S.op(aff, lambda e: e.tensor_tensor(out=r[:, :], in0=r[:, :], in1=g_bc[:, :], op=mult),
                     reads=[f"rbuf{tb}", "g_bc"], writes=[f"rbuf{tb}"])
                S.op(aff, lambda e: e.tensor_tensor(out=r[:, :], in0=r[:, :], in1=b_bc[:, :], op=add),
                     reads=[f"rbuf{tb}", "b_bc"], writes=[f"rbuf{tb}"])
                oi = S.op("pool", lambda e: e.dma_start(out=out_d[row0:row0 + 128, :], in_=r[:, :]),
                          reads=[f"rbuf{tb}"], writes=[f"out_{row0}"], dma=f"o{tb}")
                out_ops.append(oi)

            stg_a.append(stage_a); stg_b.append(stage_b); stg_c.append(stage_c)
        if T == nt_run - 1:
            tail_mode[0] = True
            for f_ in stg_b + stg_c:
                f_()
        else:
            deferred.extend(stg_a + stg_b + stg_c)

    tail_mode[0] = True
    run_deferred(10 ** 6)
    S.emit(final_wait=out_ops)
    return nc


_NC_CACHE = {}


def _host_layout(inputs):
    f = lambda a: np.ascontiguousarray(np.asarray(a, dtype=np.float32))
    w_in = f(inputs["w_in"])
    b_in = f(inputs["b_in"])
    perm = np.array(PERM)
    wp = w_in[:, perm]
    Wh = f(wp.reshape(8, 128, NG, 512).transpose(2, 1, 0, 3).reshape(NG, 128, 4096))
    bp = b_in[perm]
    shared = {
        "Wh": Wh,
        "wco": f(f(inputs["w_conv_out"]).reshape(4, 128, 1024).transpose(1, 0, 2)),
        "wao": f(f(inputs["w_attn_out"]).reshape(4, 128, 1024).transpose(1, 0, 2)),
        "wo": f(f(inputs["w_o"]).reshape(8, 128, 1024).transpose(1, 0, 2)),
        "bcol": f(bp.reshape(44, 128).T),
        "bvb": f(b_in[2560:3072].reshape(1, 512)),
        "cw": f(f(inputs["conv_w"]).reshape(CW, 4, 128).transpose(2, 1, 0)),
        "cvec": f(np.concatenate([f(inputs["conv_b"]).reshape(4, 128).T,
                                  f(inputs["conv_ln_g"]).reshape(4, 128).T,
                                  f(inputs["conv_ln_b"]).reshape(4, 128).T], axis=1)),
        "bo": f(inputs["b_o"]).reshape(1, D),
        "og": f(inputs["out_ln_g"]).reshape(1, D),
        "ob": f(inputs["out_ln_b"]).reshape(1, D),
    }
    rb = f(inputs["rel_bias"])
    k = np.arange(128)[:, None]
    q = np.arange(128)[None, :]
    idx_prev = np.minimum(q - k + 128, 128) + 128
    idx_diag = (q - k) + 128
    BT = np.empty((128, 8, 256), np.float32)
    for h in range(8):
        BT[:, h, 0:128] = rb[h][idx_prev]
        BT[:, h, 128:256] = rb[h][idx_diag]
    shared["BT"] = BT
    shared["cfar"] = f(rb[:, 256].reshape(1, 8))
    return shared


def kernel(**inputs):
    x = np.asarray(inputs["x"], dtype=np.float32)
    ncores = x.shape[0]
    if "nc" not in _NC_CACHE:
        _NC_CACHE["nc"] = build_nc(NT)
    nc = _NC_CACHE["nc"]
    shared = _host_layout(inputs)
    in_maps = []
    for b in range(ncores):
        m = dict(shared)
        m["xtok"] = np.ascontiguousarray(x[b])
        m["xT"] = np.ascontiguousarray(x[b].T)
        in_maps.append(m)
    res = run_bass_kernel_spmd(nc, in_maps, core_ids=list(range(ncores)))
    out = np.stack([np.asarray(r["out"], dtype=np.float32) for r in res.results], axis=0)
    return out
```

```python
import contextlib
import numpy as np
import concourse.bass as bass
import concourse.mybir as mybir
from concourse.bass_utils import run_bass_kernel_spmd

F32 = mybir.dt.float32
BF16 = mybir.dt.bfloat16
AF = mybir.ActivationFunctionType
ALU = mybir.AluOpType

SEQ = 4096
D = 1024
TT = 512
NT = SEQ // TT
DIN = 5632
NG = 11
LN_EPS = 1e-5
ALPHA = 2.0 ** 0.25
CW = 31
HALO = CW - 1

_R = lambda a, b: list(range(a, b))
PERM = (_R(512, 1024)
        + _R(0, 512)
        + _R(1024, 1536)
        + _R(1536, 2048)
        + _R(2048, 2560)
        + _R(2560, 3072)
        + _R(3072, 3584)
        + _R(4608, 5632)
        + _R(3584, 4608))
G_GLU, G_VAL, G_CG, G_Q, G_K, G_V, G_AG, G_GA, G_GC = 0, 1, 2, 3, 4, 5, 6, 7, 9
KPE = 16
NEG = -30000.0


class Sched:
    ENGS = ("pe", "act", "dve", "pool", "sp")

    def __init__(self, nc):
        self.nc = nc
        self.ops = []
        self.last_w = {}
        self.rd_eng = {}
        self.rd_dma = {}
        self.dma_keys = []

    def op(self, eng, fn, reads=(), writes=(), dma=None):
        i = len(self.ops)
        rs, ws = [], []
        for r in reads:
            (ws if r.startswith("ps") else rs).append(r)
        ws += list(writes)
        deps = set()
        for r in rs:
            if r in self.last_w:
                deps.add(self.last_w[r])
        for w in ws:
            if w in self.last_w:
                deps.add(self.last_w[w])
            for j in self.rd_eng.get(w, {}).values():
                deps.add(j)
            for j in self.rd_dma.get(w, ()):
                deps.add(j)
        pr = set()
        for j in deps:
            oj = self.ops[j]
            if oj["dma"] is None and dma is None and oj["eng"] == "pe" and eng == "pe":
                continue
            pr.add(j)
        for j in pr:
            self.ops[j]["signal"] = True
        self.ops.append(dict(eng=eng, fn=fn, deps=pr, dma=dma, signal=False, tok=None))
        for r in rs:
            if dma is None:
                self.rd_eng.setdefault(r, {})[eng] = i
            else:
                self.rd_dma.setdefault(r, []).append(i)
        for w in ws:
            self.last_w[w] = i
            self.rd_eng[w] = {}
            self.rd_dma[w] = []
        if dma is not None and dma not in self.dma_keys:
            self.dma_keys.append(dma)
        return i

    def emit(self, final_wait=()):
        nc = self.nc
        ops = self.ops
        for j in final_wait:
            ops[j]["signal"] = True
        with contextlib.ExitStack() as st:
            sems = {e: st.enter_context(nc.semaphore("s_" + e)) for e in self.ENGS}
            dsems = {k: st.enter_context(nc.semaphore("d_" + k)) for k in self.dma_keys}
            cnt = {e: 0 for e in self.ENGS}
            dcnt = {k: 0 for k in self.dma_keys}
            for o in ops:
                if o["dma"] is not None:
                    dcnt[o["dma"]] += 16
                    o["tok"] = (dsems[o["dma"]], dcnt[o["dma"]])
                    o["signal"] = True
                elif o["signal"]:
                    cnt[o["eng"]] += 1
                    o["tok"] = (sems[o["eng"]], cnt[o["eng"]])
            block = st.enter_context(nc.Block())

            def make(ekey):
                def body(eng):
                    waited = {}
                    for o in ops:
                        if o["eng"] != ekey:
                            continue
                        for j in sorted(o["deps"]):
                            sem, val = ops[j]["tok"]
                            if waited.get(id(sem), 0) >= val:
                                continue
                            eng.wait_ge(sem, val)
                            waited[id(sem)] = val
                        inst = o["fn"](eng)
                        if o["signal"]:
                            inst.then_inc(o["tok"][0], 16 if o["dma"] is not None else 1)
                    if ekey == "sp":
                        for j in final_wait:
                            sem, val = ops[j]["tok"]
                            if waited.get(id(sem), 0) >= val:
                                continue
                            eng.wait_ge(sem, val)
                            waited[id(sem)] = val
                return body

            block.tensor(make("pe"))
            block.scalar(make("act"))
            block.vector(make("dve"))
            block.gpsimd(make("pool"))
            block.sync(make("sp"))


def build_nc(nt_run=NT):
    nc = bass.Bass("TRN2", target_bir_lowering=False)

    def din(name, shape, dt=F32):
        return nc.dram_tensor(name, shape, dt, kind="ExternalInput").ap()

    xT_d = din("xT", [D, SEQ])
    xtok_d = din("xtok", [SEQ, D])
    Wh_d = din("Wh", [NG, 128, 4096])
    wco_d = din("wco", [128, 4, 1024])
    wao_d = din("wao", [128, 4, 1024])
    wo_d = din("wo", [128, 8, 1024])
    bcol_d = din("bcol", [128, 44])
    bvb_d = din("bvb", [1, 512])
    cw_d = din("cw", [128, 4, CW])
    cvec_d = din("cvec", [128, 12])
    BT_d = din("BT", [128, 8, 256])
    cfar_d = din("cfar", [1, 8])
    bo_d = din("bo", [1, D])
    og_d = din("og", [1, D])
    ob_d = din("ob", [1, D])
    out_d = nc.dram_tensor("out", [SEQ, D], F32, kind="ExternalOutput").ap()
    scr_d = nc.dram_tensor("scr", [NG, 128, 4096], BF16, kind="Internal").ap()

    def sb(name, shape, dt=F32):
        return nc.alloc_sbuf_tensor("s_" + name, shape, dt)

    wco = sb("wco16", [128, 4, 1024], BF16)
    wao = sb("wao16", [128, 4, 1024], BF16)
    wo = sb("wo16", [128, 8, 1024], BF16)
    wbuf = [sb(f"wbuf{i}", [128, 8, 512], BF16) for i in range(3)]
    xT16 = [sb(f"xT16_{i}", [128, 8, 512], BF16) for i in range(2)]
    uT = sb("uT", [128, 4, HALO + TT], BF16)
    Dg = sb("Dg", [128, 4, KPE, 128], BF16)
    th = [sb(f"th{i}", [128, TT], F32) for i in range(2)]
    cvb = [sb(f"cvb{i}", [128, TT], F32) for i in range(2)]
    acc = sb("acc", [128, 4, TT], F32)
    y16 = sb("y16", [128, TT], BF16)
    ysq = sb("ysq", [128, TT], BF16)
    sgate = sb("sgate", [128, 4, TT], BF16)
    qm = sb("qm", [128, 4, 4, 2, 128], BF16)
    kT = sb("kT", [128, 4, 1024], BF16)
    vaug = sb("vaug", [128, 8, 520], BF16)
    ag = sb("ag", [128, 4, TT], BF16)
    PT = [sb(f"PT{i}", [128, 1280], BF16) for i in range(2)]
    on = [sb(f"on{i}", [128, 512], BF16) for i in range(2)]
    rden = sb("rden", [128, 4], F32)
    goT = sb("goT", [128, 4, TT], BF16)
    thg = [sb(f"thg{i}", [128, TT], F32) for i in range(2)]
    t1 = [sb(f"t1_{i}", [128, TT], F32) for i in range(2)]
    hT = sb("hT", [128, 8, TT], BF16)
    rbuf = [sb(f"rbuf{i}", [128, D], F32) for i in range(4)]
    g_bc = sb("g_bc", [128, D], F32)
    b_bc = sb("b_bc", [128, D], F32)
    bvb_hl = sb("bvb_hl", [128, 512], BF16)
    Bsc = sb("Bsc", [128, 4, 2, 2, 128], BF16)
    mean_sb = sb("mean_sb", [128, TT], F32)
    bcol = sb("bcol", [128, 44], F32)
    hb = sb("hb", [128, 44], F32)
    cw2 = sb("cw2", [128, 4, CW], F32)
    cvec = sb("cvec", [128, 12], F32)
    cfar = sb("cfar", [128, 8], F32)
    maskT = sb("maskT", [128, 2, 128], BF16)
    epst = sb("epst", [128, 1], F32)
    mhalf = sb("mhalf", [128, 1], F32)
    ident = sb("ident", [128, 128], BF16)
    identf = sb("identf", [128, 128], F32)
    onesc = sb("onesc", [128, 128], BF16)
    bo_hl = sb("bo_hl", [128, D], BF16)
    ones33 = sb("ones33", [128, 128], BF16)
    st = sb("st", [128, 32], F32)
    bst = sb("bst", [128, 4, 12], F32)

    junk = mean_sb[:, :].bitcast(BF16)
    b33, d33, hi33 = rbuf[0][0:33, :], rbuf[1][0:33, :], rbuf[2][0:33, :].bitcast(BF16)[:, 0:D]
    var_sb, rstd_sb = t1[0], t1[1]

    scs = [nc.alloc_psum_tensor(f"sc{i}", [128, 1536], F32) for i in range(2)]
    _b6 = nc.alloc_psum_tensor("bank6", [128, 512], F32)
    _b7 = nc.alloc_psum_tensor("bank7", [128, 512], F32)
    banks = [scs[0][:, 0:512], scs[0][:, 512:1024], scs[0][:, 1024:1536],
             scs[1][:, 0:512], scs[1][:, 512:1024], scs[1][:, 1024:1536], _b6[:, :], _b7[:, :]]

    S = Sched(nc)
    mult, add, sub = ALU.mult, ALU.add, ALU.subtract

    def xT_load(T, extra_reads=()):
        t0 = T * TT
        src = xT_d.rearrange("(kc p) t -> p kc t", p=128)[:, :, t0:t0 + TT]
        S.op("pool", lambda e, T=T, src=src: e.dma_start(out=xT16[T % 2][:, :, :], in_=src),
             reads=list(extra_reads), writes=[f"xT{T % 2}"], dma=f"xT{T % 2}")

    def scr_cast(g):
        src = Wh_d[g].rearrange("p (a b) -> p a b", b=2048)
        dst = scr_d[g].rearrange("p (a b) -> p a b", b=2048)
        S.op("pool", lambda e, src=src, dst=dst: e.dma_start(out=dst, in_=src),
             reads=[], writes=[f"scr{g}"], dma=f"scr{g}")

    xT_load(0)

    def sp_load(dst, src, key):
        S.op("sp", lambda e: e.dma_start(out=dst, in_=src), reads=[], writes=[key], dma="ld_" + key)

    sp_load(bcol[:, :], bcol_d[:, :], "bcol")
    sp_load(cw2[:, :, :], cw_d[:, :, :], "cw2")
    sp_load(cvec[:, :], cvec_d[:, :], "cvec")
    sp_load(cfar[:, :], cfar_d[0:1, :].broadcast_to([128, 8]), "cfar")
    BTf = acc[:, :, :].rearrange("p a (b c) -> p (a b) c", c=256)
    S.op("sp", lambda e: e.dma_start(out=BTf, in_=BT_d[:, :, :]), reads=[], writes=[f"acc{k}" for k in range(4)], dma="ld_BT")
    sp_load(g_bc[:, :], og_d[0:1, :].broadcast_to([128, D]), "g_bc")
    sp_load(b_bc[:, :], ob_d[0:1, :].broadcast_to([128, D]), "b_bc")
    S.op("pool", lambda e: e.memset(identf[:, :], 0.0), writes=["identf"])
    S.op("pool", lambda e: e.affine_select(out=identf[:, :], in_=identf[:, :], pattern=[[-1, 128]],
                                           compare_op=ALU.not_equal, fill=1.0, base=0, channel_multiplier=1),
         reads=["identf"], writes=["identf"])
    S.op("pool", lambda e: e.memset(b33[:, :], 0.0), writes=["rbuf0"])
    S.op("sp", lambda e: e.dma_start(out=b33[0:1, :], in_=bo_d[0:1, :]), reads=[], writes=["rbuf0"], dma="ld_b33a")
    S.op("sp", lambda e: e.dma_start(out=b33[32:33, :], in_=bo_d[0:1, :]), reads=[], writes=["rbuf0"], dma="ld_b33b")

    S.op("pool", lambda e: e.memset(uT[:, :, :], 0.0), writes=[f"uT{k}" for k in range(4)])
    S.op("pool", lambda e: e.memset(vaug[:, :, :], 1.0), writes=[f"va{s}" for s in range(8)])
    S.op("pool", lambda e: e.memset(epst[:, :], LN_EPS), writes=["epst"])
    S.op("pool", lambda e: e.memset(mhalf[:, :], -0.5), writes=["mhalf"])
    S.op("pool", lambda e: e.memset(onesc[:, :], 1.0 / 512.0), writes=["onesc"])
    S.op("pool", lambda e: e.memset(ones33[:, :], 1.0), writes=["ones33"])
    S.op("dve", lambda e: e.tensor_copy(out=ident[:, :], in_=identf[:, :]), reads=["identf"], writes=["ident"])
    S.op("dve", lambda e: e.tensor_scalar(out=hb[:, :], in0=bcol[:, :], scalar1=0.5, scalar2=None, op0=mult),
         reads=["bcol"], writes=["hb"])
    S.op("dve", lambda e: e.tensor_scalar(out=cw2[:, :, :], in0=cw2[:, :, :], scalar1=0.5, scalar2=None, op0=mult),
         reads=["cw2"], writes=["cw2"])
    def build_dg(kc):
        for j in range(KPE):
            S.op("act", lambda e, kc=kc, j=j: e.activation(out=Dg[:, kc, j, :], in_=identf[:, :], func=AF.Copy,
                                                           scale=cw2[:, kc, j:j + 1]),
                 reads=["identf", "cw2"], writes=[f"Dg{kc}"])

    build_dg(0)
    S.op("dve", lambda e: e.memset(maskT[:, :, :], 0.0), writes=["maskT"])
    S.op("dve", lambda e: e.memset(maskT[0:64, :, 64:128], NEG), reads=["maskT"], writes=["maskT"])
    for h_ in range(8):
        for jj in range(2):
            S.op("dve", lambda e, h_=h_, jj=jj: e.tensor_scalar(
                out=Bsc[:, h_ // 2, jj, h_ % 2, :], in0=BTf[:, h_, jj * 128:(jj + 1) * 128],
                scalar1=cfar[:, h_:h_ + 1], scalar2=8.0, op0=sub, op1=mult),
                reads=[f"acc{k}" for k in range(4)] + ["cfar"], writes=["Bsc"])
    for hc_ in range(4):
        S.op("dve", lambda e, hc_=hc_: e.memset(Bsc[64:128, hc_, 1, :, 0:64], NEG), reads=["Bsc"], writes=["Bsc"])
    S.op("pool", lambda e: e.memset(qm[:, :, :, :, :].rearrange("p a b c d -> p (a b c d)"), 0.0), writes=[f"qm{k}" for k in range(4)])
    S.op("dve", lambda e: e.tensor_copy(out=hi33, in_=b33[:, :]), reads=["rbuf0"], writes=["rbuf2"])
    S.op("dve", lambda e: e.tensor_tensor(out=d33[:, :], in0=b33[:, :], in1=hi33, op=sub),
         reads=["rbuf0", "rbuf2"], writes=["rbuf1"])
    S.op("dve", lambda e: e.memset(bo_hl[:, :], 0.0), writes=["bo_hl"])
    S.op("dve", lambda e: e.tensor_copy(out=bo_hl[0:33, :], in_=hi33), reads=["rbuf2", "bo_hl"], writes=["bo_hl"])
    S.op("dve", lambda e: e.tensor_copy(out=bo_hl[32:33, :], in_=d33[32:33, :]), reads=["rbuf1", "bo_hl"], writes=["bo_hl"])
    bv_st, bv_df, bv_hi = rbuf[3][0:33, 0:512], rbuf[3][0:33, 512:1024], junk[0:33, 0:512]
    S.op("pool", lambda e: e.memset(rbuf[3][0:33, :], 0.0), writes=["rbuf3"])
    S.op("sp", lambda e: e.dma_start(out=rbuf[3][0:1, 0:512], in_=bvb_d[0:1, :]), reads=[], writes=["rbuf3"], dma="ld_bv_a")
    S.op("sp", lambda e: e.dma_start(out=rbuf[3][32:33, 0:512], in_=bvb_d[0:1, :]), reads=[], writes=["rbuf3"], dma="ld_bv_b")
    S.op("dve", lambda e: e.tensor_copy(out=bv_hi, in_=bv_st), reads=["rbuf3"], writes=["mean_sb"])
    S.op("dve", lambda e: e.tensor_tensor(out=bv_df, in0=bv_st, in1=bv_hi, op=sub), reads=["rbuf3", "mean_sb"], writes=["rbuf3"])
    S.op("dve", lambda e: e.memset(bvb_hl[:, :], 0.0), writes=["bvb_hl"])
    S.op("dve", lambda e: e.tensor_copy(out=bvb_hl[0:33, :], in_=bv_hi), reads=["mean_sb", "bvb_hl"], writes=["bvb_hl"])
    S.op("dve", lambda e: e.tensor_copy(out=bvb_hl[32:33, :], in_=rbuf[3][32:33, 512:1024]), reads=["rbuf3", "bvb_hl"], writes=["bvb_hl"])

    zb = [0]

    def next_bank(choices=(0, 1, 6, 7)):
        b = choices[zb[0] % len(choices)]
        zb[0] += 1
        return b

    wl_next = [0]
    n_groups_total = nt_run * NG

    def wload_upto(n_max):
        while wl_next[0] < min(n_max, n_groups_total):
            n = wl_next[0]
            g = n % NG
            slot = n % 3
            if n < NG:
                dst = wbuf[slot][:, :, :].rearrange("p k c -> p (k c)").rearrange("p (a b) -> p a b", b=2048)
                src = Wh_d[g].rearrange("p (a b) -> p a b", b=2048)
                S.op("pool", lambda e, dst=dst, src=src: e.dma_start(out=dst, in_=src),
                     reads=[], writes=[f"wbuf{slot}"], dma=f"wbq{slot}")
            else:
                S.op("sp", lambda e, g=g, slot=slot: e.dma_start(out=wbuf[slot][:, :, :],
                                                                 in_=scr_d[g].rearrange("p (k c) -> p k c", c=512)),
                     reads=[f"scr{g}"], writes=[f"wbuf{slot}"], dma=f"wb{slot}")
            wl_next[0] += 1

    def zchunk(T, g, ec, bank):
        n = T * NG + g
        slot = n % 3
        xb = T % 2

        def fn(e):
            for kc in range(8):
                last = e.matmul(banks[bank][:, :], lhsT=wbuf[slot][:, kc, ec * 128:(ec + 1) * 128],
                                rhs=xT16[xb][:, kc, :], start=(kc == 0), stop=(kc == 7))
            return last
        S.op("pe", fn, reads=[f"wbuf{slot}", f"xT{xb}"], writes=[f"ps{bank}"])

    def group_done(T, g):
        wload_upto(T * NG + g + 4)
        if T == 0 and nt_run > 1:
            scr_pending.append(g)

    deferred = []
    scr_pending = []

    def scr_flush(limit=None):
        while scr_pending and (limit is None or scr_pending[0] < limit):
            scr_cast(scr_pending.pop(0))
    tail_mode = [False]

    def run_deferred(n):
        while n > 0 and deferred:
            deferred.pop(0)()
            n -= 1

    out_ops = []
    wload_upto(3)
    for T in range(nt_run):
        t0 = T * TT
        xb = T % 2

        conv_nj = [KPE] * 4
        conv_ready = [False] * 4
        conv_last = [3]

        def conv_left():
            return sum(CW - x for x in conv_nj)

        def conv_pump(n):
            while n > 0:
                pick = None
                for d_ in range(1, 5):
                    kc = (conv_last[0] + d_) % 4
                    if conv_ready[kc] and conv_nj[kc] < CW:
                        pick = kc
                        break
                if pick is None:
                    return
                kc = pick
                j = conv_nj[kc]
                conv_nj[kc] += 1
                conv_last[0] = kc
                n -= 1
                if j == KPE:
                    cb = 2 + kc
                    S.op("dve", lambda e, kc=kc, j=j, cb=cb: e.scalar_tensor_tensor(
                        out=acc[:, kc, :], in0=uT[:, kc, j:j + TT], scalar=cw2[:, kc, j:j + 1], in1=banks[cb][:, :],
                        op0=mult, op1=add), reads=[f"uT{kc}", "cw2", f"ps{cb}"], writes=[f"acc{kc}"])
                else:
                    S.op("dve", lambda e, kc=kc, j=j: e.scalar_tensor_tensor(
                        out=acc[:, kc, :], in0=uT[:, kc, j:j + TT], scalar=cw2[:, kc, j:j + 1], in1=acc[:, kc, :],
                        op0=mult, op1=add), reads=[f"uT{kc}", "cw2", f"acc{kc}"], writes=[f"acc{kc}"])

        def pe_taps(kc):
            cb = 2 + kc

            def fconv(e):
                for j in range(KPE):
                    last = e.matmul(banks[cb][:, :], lhsT=Dg[:, kc, j, :], rhs=uT[:, kc, j:j + TT],
                                    start=(j == 0), stop=(j == KPE - 1))
                return last
            S.op("pe", fconv, reads=[f"Dg{kc}", f"uT{kc}"], writes=[f"ps{cb}"])
            conv_ready[kc] = True
            conv_pump(2)

        for ec in range(4):
            bA = next_bank()
            zchunk(T, G_GLU, ec, bA)
            jb = G_GLU * 4 + ec
            S.op("act", lambda e, bA=bA, jb=jb, ec=ec: e.activation(
                out=th[ec % 2][:, :], in_=banks[bA][:, :], func=AF.Tanh, bias=hb[:, jb:jb + 1], scale=0.5),
                reads=[f"ps{bA}", "hb"], writes=[f"th{ec % 2}"])
            bB = next_bank()
            zchunk(T, G_VAL, ec, bB)
            jv = G_VAL * 4 + ec
            S.op("act", lambda e, bB=bB, jv=jv, ec=ec: e.activation(
                out=cvb[ec % 2][:, :], in_=banks[bB][:, :], func=AF.Identity, bias=bcol[:, jv:jv + 1], scale=1.0),
                reads=[f"ps{bB}", "bcol"], writes=[f"cvb{ec % 2}"])
            S.op("dve", lambda e, ec=ec: e.scalar_tensor_tensor(
                out=uT[:, ec, HALO:HALO + TT], in0=th[ec % 2][:, :], scalar=1.0, in1=cvb[ec % 2][:, :],
                op0=add, op1=mult), reads=[f"th{ec % 2}", f"cvb{ec % 2}"], writes=[f"uT{ec}"])
            if T == 0 and ec < 3:
                build_dg(ec + 1)
            if ec >= 2:
                pe_taps(ec - 2)
            run_deferred(2)
        group_done(T, G_GLU)
        group_done(T, G_VAL)
        for ec in range(4):
            bk = next_bank()
            zchunk(T, G_CG, ec, bk)
            jb = G_CG * 4 + ec
            S.op("act", lambda e, bk=bk, jb=jb, ec=ec: e.activation(
                out=sgate[:, ec, :], in_=banks[bk][:, :], func=AF.Silu, bias=bcol[:, jb:jb + 1], scale=1.0),
                reads=[f"ps{bk}", "bcol"], writes=[f"sgate{ec}"])
            if ec == 0:
                pe_taps(2)
            elif ec == 1:
                pe_taps(3)
            else:
                conv_pump(3)
            run_deferred(2)
        group_done(T, G_CG)
        for hc in range(4):
            bk = next_bank((0, 1, 6, 7, 2, 3, 4, 5))
            zchunk(T, G_Q, hc, bk)
            jb = G_Q * 4 + hc
            for hh_ in range(2):
                pl = slice(hh_ * 64, hh_ * 64 + 64)
                S.op("act", lambda e, bk=bk, jb=jb, hc=hc, hh_=hh_, pl=pl: e.activation(
                    out=qm[pl, hc, :, hh_, :], in_=banks[bk][pl, :].rearrange("p (b q) -> p b q", q=128),
                    func=AF.Identity, bias=bcol[pl, jb:jb + 1], scale=1.0),
                    reads=[f"ps{bk}", "bcol"], writes=[f"qm{hc}"])
            conv_pump(2)
            run_deferred(2)
        group_done(T, G_Q)
        rc = (t0 % 1024)
        for hc in range(4):
            bk = next_bank((0, 1, 6, 7, 2, 3, 4, 5))
            zchunk(T, G_K, hc, bk)
            jb = G_K * 4 + hc
            S.op("act", lambda e, bk=bk, jb=jb, hc=hc, rc=rc: e.activation(
                out=kT[:, hc, rc:rc + TT], in_=banks[bk][:, :], func=AF.Identity, bias=bcol[:, jb:jb + 1], scale=1.0),
                reads=[f"ps{bk}", "bcol"], writes=[f"kT{hc}_{(4 * T + i) % 8}" for i in range(4)])
            conv_pump(2)
            run_deferred(2)
        group_done(T, G_K)
        n_v = T * NG + G_V
        slot_v = n_v % 3
        for tb in range(4):
            bk = next_bank((0, 1, 6, 7, 2, 3, 4, 5))

            def fnv(e, tb=tb, bk=bk, xb=xb, slot_v=slot_v):
                for kc in range(8):
                    e.matmul(banks[bk][:, :], lhsT=xT16[xb][:, kc, tb * 128:(tb + 1) * 128],
                             rhs=wbuf[slot_v][:, kc, :], start=(kc == 0), stop=False)
                return e.matmul(banks[bk][:, :], lhsT=ones33[:, :], rhs=bvb_hl[:, :], start=False, stop=True)
            S.op("pe", fnv, reads=[f"wbuf{slot_v}", f"xT{xb}", "ones33", "bvb_hl"], writes=[f"ps{bk}"])
            vs = (4 * T + tb) % 8
            S.op("act", lambda e, bk=bk, vs=vs: e.activation(
                out=vaug[:, vs, :].rearrange("p (h d) -> p h d", d=65)[:, :, 0:64],
                in_=banks[bk][:, :].rearrange("p (h d) -> p h d", d=64), func=AF.Copy),
                reads=[f"ps{bk}"], writes=[f"va{vs}"])
            conv_pump(2)
            run_deferred(2)
        group_done(T, G_V)
        for hc in range(4):
            bk = next_bank((0, 1, 6, 7, 2, 3, 4, 5))
            zchunk(T, G_AG, hc, bk)
            jb = G_AG * 4 + hc
            S.op("act", lambda e, bk=bk, jb=jb, hc=hc: e.activation(
                out=ag[:, hc, :], in_=banks[bk][:, :], func=AF.Silu, bias=bcol[:, jb:jb + 1], scale=1.0),
                reads=[f"ps{bk}", "bcol"], writes=[f"ag{hc}"])
            conv_pump(2)
            run_deferred(2)
        group_done(T, G_AG)
        run_deferred(10 ** 6)
        for tb in range(4):
            row0 = t0 + tb * 128
            S.op("sp", lambda e, tb=tb, row0=row0: e.dma_start(out=rbuf[tb][:, :], in_=xtok_d[row0:row0 + 128, :]),
                 reads=[], writes=[f"rbuf{tb}"], dma=f"xr{tb}")

        if T == 0:
            S.op("pool", lambda e: e.dma_start(out=wao[:, :, :], in_=wao_d[:, :, :]), reads=[], writes=["wao"], dma="ld_wao")
            S.op("pool", lambda e: e.dma_start(out=wco[:, :, :], in_=wco_d[:, :, :]), reads=[], writes=["wco"], dma="ld_wco")
            S.op("pool", lambda e: e.dma_start(out=wo[:, :, :], in_=wo_d[:, :, :]), reads=[], writes=["wo"], dma="ld_wo")
        if T + 1 < nt_run:
            xT_load(T + 1)
        scr_flush(3)
        pairs = [(bl, hc) for bl in range(4) for hc in range(4)]
        SCOL = lambda j: 256 + 256 * j

        def emit_qk(p):
            bl, hc = pairs[p]
            b = 4 * T + bl
            st_ = p % 2
            sc = scs[st_]
            jmin = max(0, 4 - b)
            q2 = qm[:, hc, bl, :, :].rearrange("p a b -> p (a b)")

            def fn(e):
                for j in range(jmin, 5):
                    kt = b - 4 + j
                    col = (kt % 8) * 128
                    o = sc[:, SCOL(j):SCOL(j) + 256]
                    last = e.matmul(o, lhsT=kT[:, hc, col:col + 128], rhs=q2, start=True, stop=(j in (1, 2)))
                    if j >= 3:
                        last = e.matmul(o, lhsT=ident[:, :], rhs=Bsc[:, hc, j - 3, :, :].rearrange("p a b -> p (a b)"),
                                        start=False, stop=True)
                    if j == 0:
                        last = e.matmul(o, lhsT=ident[:, :], rhs=maskT[:, :, :].rearrange("p a b -> p (a b)"),
                                        start=False, stop=True)
                return last
            rds = [f"qm{hc}", "ident", "Bsc", "maskT"] + [f"kT{hc}_{(b - 4 + j) % 8}" for j in range(jmin, 5)]
            S.op("pe", fn, reads=rds, writes=[f"ps{3 * st_ + i}" for i in range(3)])

        def emit_softmax(p):
            bl, hc = pairs[p]
            b = 4 * T + bl
            st_ = p % 2
            jmin = max(0, 4 - b)
            c0 = SCOL(jmin)
            S.op("act", lambda e: e.activation(out=PT[st_][:, c0 - 256:1280], in_=scs[st_][:, c0:1536], func=AF.Exp,
                                               scale=0.125),
                 reads=[f"ps{3 * st_ + i}" for i in range(3)], writes=[f"PT{st_}"])

        def emit_pv(p):
            bl, hc = pairs[p]
            b = 4 * T + bl
            st_ = p % 2
            P = PT[st_]
            sl = lambda j: (b - 4 + j) % 8
            js = [j for j in range(5) if b - 4 + j >= 0]
            half = hc // 2
            ob = banks[6 + half]
            mms = []
            for hh in range(2):
                h = 2 * hc + hh
                c = (h % 4) * 65
                for i, j in enumerate(js):
                    mms.append((ob[:, c:c + 65], P[:, j * 256 + hh * 128:j * 256 + hh * 128 + 128],
                                vaug[:, sl(j), h * 65:(h + 1) * 65], i == 0, i == len(js) - 1))

            def fn(e):
                for (o, l, r, st0, sp0) in mms:
                    last = e.matmul(o, lhsT=l, rhs=r, start=st0, stop=sp0)
                return last
            S.op("pe", fn, reads=[f"PT{st_}"] + [f"va{sl(j)}" for j in js], writes=[f"ps{6 + half}"])

        def emit_norm(bl, half):
            ov = banks[6 + half][:, 0:260].rearrange("p (h d) -> p h d", d=65)
            os_ = bl % 2
            S.op("dve", lambda e: e.reciprocal(out=rden[:, :].unsqueeze(2), in_=ov[:, :, 64:65]),
                 reads=[f"ps{6 + half}"], writes=["rden"])
            S.op("dve", lambda e: e.tensor_tensor(
                out=on[os_][:, half * 256:(half + 1) * 256].rearrange("p (h d) -> p h d", d=64),
                in0=ov[:, :, 0:64], in1=rden[:, :].unsqueeze(2).broadcast_to([128, 4, 64]), op=mult),
                reads=[f"ps{6 + half}", "rden"], writes=[f"on{os_}_{half}"])

        def emit_transpose(bl, tbk):
            os_ = bl % 2
            b16 = banks[tbk][:, 0:256].bitcast(BF16)

            def fn(e):
                for hc in range(4):
                    last = e.transpose(out=b16[:, hc * 128:(hc + 1) * 128], in_=on[os_][:, hc * 128:(hc + 1) * 128],
                                       identity=ident[:, :])
                return last
            S.op("pe", fn, reads=[f"on{os_}_0", f"on{os_}_1", "ident"], writes=[f"ps{tbk}"])
            S.op("dve", lambda e: e.tensor_tensor(
                out=goT[:, :, bl * 128:(bl + 1) * 128], in0=b16[:, 0:512].rearrange("p (c q) -> p c q", q=128),
                in1=ag[:, :, bl * 128:(bl + 1) * 128], op=mult),
                reads=[f"ps{tbk}"] + [f"ag{i}" for i in range(4)], writes=["goT"])

        npairs = len(pairs)
        emit_qk(0)
        emit_softmax(0)
        pend_tr = None
        for p in range(npairs):
            if p + 1 < npairs:
                emit_qk(p + 1)
                emit_softmax(p + 1)
            emit_pv(p)
            bl, hc = pairs[p]
            if hc == 1:
                emit_norm(bl, 0)
            if hc == 3:
                emit_norm(bl, 1)
                pend_tr = bl
            if pend_tr is not None and (hc == 2 or p == npairs - 1):
                emit_transpose(pend_tr, 3 * (p % 2))
                pend_tr = None
            conv_pump(2)

        conv_pump(10 ** 6)
        for kc in range(4):
            S.op("dve", lambda e, kc=kc: e.tensor_copy(out=uT[:, kc, 0:HALO], in_=uT[:, kc, TT:TT + HALO]),
                 reads=[f"uT{kc}"], writes=[f"uT{kc}"])
        def ln_stats(kc):
            S.op("act", lambda e, kc=kc: e.activation(out=y16[:, :], in_=acc[:, kc, :], func=AF.Identity,
                                                      bias=cvec[:, kc:kc + 1], scale=1.0),
                 reads=[f"acc{kc}", "cvec"], writes=["y16"])
            S.op("act", lambda e, kc=kc: e.activation(out=ysq[:, :], in_=acc[:, kc, :], func=AF.Square,
                                                      bias=cvec[:, kc:kc + 1], scale=1.0),
                 reads=[f"acc{kc}", "cvec"], writes=["ysq"])
            S.op("pe", lambda e, kc=kc: e.matmul(banks[4][:, :], lhsT=onesc[:, :], rhs=y16[:, :],
                                                 start=(kc == 0), stop=(kc == 3)),
                 reads=["onesc", "y16"], writes=["ps4"])
            S.op("pe", lambda e, kc=kc: e.matmul(banks[5][:, :], lhsT=onesc[:, :], rhs=ysq[:, :],
                                                 start=(kc == 0), stop=(kc == 3)),
                 reads=["onesc", "ysq"], writes=["ps5"])

        def ln_rstd_a():
            S.op("act", lambda e: e.activation(out=mean_sb[:, :], in_=banks[4][:, :], func=AF.Copy),
                 reads=["ps4"], writes=["mean_sb"])
            S.op("dve", lambda e: e.scalar_tensor_tensor(out=var_sb[:, :], in0=mean_sb[:, :], scalar=-1.0, in1=mean_sb[:, :],
                                                         op0=mult, op1=mult), reads=["mean_sb"], writes=["t1_0"])
            S.op("dve", lambda e: e.tensor_tensor(out=var_sb[:, :], in0=banks[5][:, :], in1=var_sb[:, :], op=add),
                 reads=["ps5", "t1_0"], writes=["t1_0"])

        def ln_rstd_b():
            S.op("act", lambda e: e.activation(out=rstd_sb[:, :], in_=var_sb[:, :], func=AF.Sqrt, bias=epst[:, 0:1], scale=1.0),
                 reads=["t1_0", "epst"], writes=["t1_1"])
            S.op("dve", lambda e: e.reciprocal(out=rstd_sb[:, :], in_=rstd_sb[:, :]), reads=["t1_1"], writes=["t1_1"])

        def ln_apply(kc):
            S.op("dve", lambda e, kc=kc: e.scalar_tensor_tensor(out=acc[:, kc, :], in0=acc[:, kc, :],
                                                                scalar=cvec[:, kc:kc + 1], in1=mean_sb[:, :],
                                                                op0=add, op1=sub),
                 reads=[f"acc{kc}", "mean_sb", "cvec"], writes=[f"acc{kc}"])
            S.op("dve", lambda e, kc=kc: e.tensor_tensor(out=acc[:, kc, :], in0=acc[:, kc, :], in1=rstd_sb[:, :], op=mult),
                 reads=[f"acc{kc}", "t1_1"], writes=[f"acc{kc}"])

        def ln_silu(kc):
            S.op("act", lambda e, kc=kc: e.activation(out=acc[:, kc, :], in_=acc[:, kc, :], func=AF.Silu,
                                                      scale=cvec[:, 4 + kc:5 + kc], bias=cvec[:, 8 + kc:9 + kc]),
                 reads=[f"acc{kc}", "cvec"], writes=[f"acc{kc}"])
            S.op("pool", lambda e, kc=kc: e.tensor_tensor(out=sgate[:, kc, :], in0=acc[:, kc, :], in1=sgate[:, kc, :], op=mult),
                 reads=[f"acc{kc}", f"sgate{kc}"], writes=[f"sgate{kc}"])

        ln_steps = [lambda: (ln_stats(0), ln_stats(1)), lambda: (ln_stats(2), ln_stats(3)), ln_rstd_a, ln_rstd_b,
                    lambda: (ln_apply(0), ln_apply(1)), lambda: (ln_apply(2), ln_apply(3), ln_silu(0), ln_silu(1)),
                    lambda: (ln_silu(2), ln_silu(3))]

        if T == 0:
            S.op("dve", lambda e: e.tensor_scalar(out=wo[:, :, :], in0=wo[:, :, :], scalar1=0.5, scalar2=None, op0=mult),
                 reads=["wo"], writes=["wo"])
        lb = [0]

        def late_bank():
            b_ = (0, 1, 2, 3)[lb[0] % 4]
            lb[0] += 1
            return b_

        for dc in range(8):
            g_a = G_GA + dc // 4
            ec = dc % 4
            bk = late_bank()
            zchunk(T, g_a, ec, bk)
            ja = g_a * 4 + ec
            ta = thg[dc % 2]
            S.op("act", lambda e, bk=bk, ja=ja, ta=ta: e.activation(out=ta[:, :], in_=banks[bk][:, :], func=AF.Tanh,
                                                                    bias=hb[:, ja:ja + 1], scale=0.5),
                 reads=[f"ps{bk}", "hb"], writes=[f"thg{dc % 2}"])
            bk2 = late_bank()

            def fao(e, bk2=bk2, dc=dc):
                for kc in range(4):
                    last = e.matmul(banks[bk2][:, :], lhsT=wao[:, kc, dc * 128:(dc + 1) * 128], rhs=goT[:, kc, :],
                                    start=(kc == 0), stop=(kc == 3))
                return last
            S.op("pe", fao, reads=["wao", "goT"], writes=[f"ps{bk2}"])
            S.op("dve", lambda e, bk2=bk2, ta=ta, dc=dc: e.scalar_tensor_tensor(
                out=hT[:, dc, :], in0=ta[:, :], scalar=1.0, in1=banks[bk2][:, :], op0=add, op1=mult),
                reads=[f"ps{bk2}", f"thg{dc % 2}"], writes=[f"hT{dc}"])
            if ln_steps:
                ln_steps.pop(0)()
            if ec == 3:
                group_done(T, g_a)

        while ln_steps:
            ln_steps.pop(0)()
        scr_flush()
        gate_bank = {}

        def gate_part(dc):
            g_c = G_GC + dc // 4
            ec = dc % 4
            bk3 = late_bank()
            gate_bank[dc] = bk3
            zchunk(T, g_c, ec, bk3)
            jc = g_c * 4 + ec
            tc_ = thg[dc % 2]
            S.op("act", lambda e: e.activation(out=tc_[:, :], in_=banks[bk3][:, :], func=AF.Tanh,
                                               bias=hb[:, jc:jc + 1], scale=0.5),
                 reads=[f"ps{bk3}", "hb"], writes=[f"thg{dc % 2}"])
            if ec == 3:
                group_done(T, g_c)

        def co_part(dc):
            tc_ = thg[dc % 2]
            bk4 = late_bank()

            def fco(e):
                for kc in range(4):
                    last = e.matmul(banks[bk4][:, :], lhsT=wco[:, kc, dc * 128:(dc + 1) * 128], rhs=sgate[:, kc, :],
                                    start=(kc == 0), stop=(kc == 3))
                return last
            S.op("pe", fco, reads=["wco"] + [f"sgate{k}" for k in range(4)], writes=[f"ps{bk4}"])
            tt_ = t1[dc % 2]
            S.op("dve", lambda e: e.scalar_tensor_tensor(
                out=tt_[:, :], in0=tc_[:, :], scalar=1.0, in1=banks[bk4][:, :], op0=add, op1=mult),
                reads=[f"ps{bk4}", f"thg{dc % 2}"], writes=[f"t1_{dc % 2}"])
            S.op("pool", lambda e: e.tensor_tensor(out=hT[:, dc, :], in0=tt_[:, :], in1=hT[:, dc, :], op=add),
                 reads=[f"t1_{dc % 2}", f"hT{dc}"], writes=[f"hT{dc}"])

        gate_part(0)
        for dc in range(8):
            if dc + 1 < 8:
                gate_part(dc + 1)
            co_part(dc)

        scr_flush()
        stg_a, stg_b, stg_c = [], [], []
        for tb in range(4):
            r = rbuf[tb]
            row0 = t0 + tb * 128
            yb = (4, 6)[tb % 2]
            for hf in range(2):
                def fy1(e, tb=tb, hf=hf, yb=yb):
                    for kc in range(6):
                        last = e.matmul(banks[yb + hf][:, :], lhsT=hT[:, kc, tb * 128:(tb + 1) * 128],
                                        rhs=wo[:, kc, hf * 512:(hf + 1) * 512], start=(kc == 0), stop=False)
                    return last

                def fy2(e, tb=tb, hf=hf, yb=yb):
                    for kc in range(6, 8):
                        e.matmul(banks[yb + hf][:, :], lhsT=hT[:, kc, tb * 128:(tb + 1) * 128],
                                 rhs=wo[:, kc, hf * 512:(hf + 1) * 512], start=False, stop=False)
                    return e.matmul(banks[yb + hf][:, :], lhsT=ones33[:, :], rhs=bo_hl[:, hf * 512:(hf + 1) * 512],
                                    start=False, stop=True)
                S.op("pe", fy1, reads=["wo"] + [f"hT{k}" for k in range(6)], writes=[f"ps{yb + hf}"])
                S.op("pe", fy2, reads=["wo", "ones33", "bo_hl", "hT6", "hT7"], writes=[f"ps{yb + hf}"])
                S.op("dve", lambda e, r=r, hf=hf, yb=yb: e.scalar_tensor_tensor(
                    out=r[:, hf * 512:(hf + 1) * 512], in0=r[:, hf * 512:(hf + 1) * 512], scalar=ALPHA,
                    in1=banks[yb + hf][:, :], op0=mult, op1=add),
                    reads=[f"ps{yb + hf}", f"rbuf{tb}"], writes=[f"rbuf{tb}"])
                S.op("dve", lambda e, r=r, hf=hf, tb=tb: e.bn_stats(out=bst[:, tb, hf * 6:(hf + 1) * 6],
                                                                    in_=r[:, hf * 512:(hf + 1) * 512]),
                     reads=[f"rbuf{tb}"], writes=[f"bst{tb}_{hf}"])
            so_ = tb * 8
            S.op("dve", lambda e, tb=tb, so_=so_: e.bn_aggr(out=st[:, so_ + 2:so_ + 4], in_=bst[:, tb, :]),
                 reads=[f"bst{tb}_0", f"bst{tb}_1"], writes=[f"st{tb}c"])

            def stage_a(tb=tb, r=r):
                return

            def stage_b(tb=tb, r=r):
                so = tb * 8
                S.op("pool", lambda e: e.tensor_scalar(out=st[:, so + 4:so + 5], in0=st[:, so + 3:so + 4], scalar1=1.0,
                                                       scalar2=LN_EPS, op0=mult, op1=add),
                     reads=[f"st{tb}c"], writes=[f"st{tb}e"])
                S.op("pool", lambda e: e.tensor_tensor(out=st[:, so + 5:so + 6], in0=st[:, so + 4:so + 5],
                                                       in1=mhalf[:, 0:1], op=ALU.pow),
                     reads=[f"st{tb}e", "mhalf"], writes=[f"st{tb}f"])
                S.op("dve", lambda e: e.tensor_scalar(out=st[:, so + 6:so + 7], in0=st[:, so + 2:so + 3],
                                                      scalar1=st[:, so + 5:so + 6], scalar2=-1.0, op0=mult, op1=mult),
                     reads=[f"st{tb}c", f"st{tb}f"], writes=[f"st{tb}g"])

            def stage_c(tb=tb, r=r, row0=row0):
                so = tb * 8
                S.op("act", lambda e: e.activation(out=r[:, :], in_=r[:, :], func=AF.Identity,
                                                   scale=st[:, so + 5:so + 6], bias=st[:, so + 6:so + 7]),
                     reads=[f"rbuf{tb}", f"st{tb}f", f"st{tb}g"], writes=[f"rbuf{tb}"])
                aff = "dve" if (tail_mode[0] and tb >= 1) else "pool"
                S.op(aff, lambda e: e.tensor_tensor(out=r[:, :], in0=r[:, :], in1=g_bc[:, :], op=mult),
                     reads=[f"rbuf{tb}", "g_bc"], writes=[f"rbuf{tb}"])
                S.op(aff, lambda e: e.tensor_tensor(out=r[:, :], in0=r[:, :], in1=b_bc[:, :], op=add),
                     reads=[f"rbuf{tb}", "b_bc"], writes=[f"rbuf{tb}"])
                oi = S.op("pool", lambda e: e.dma_start(out=out_d[row0:row0 + 128, :], in_=r[:, :]),
                          reads=[f"rbuf{tb}"], writes=[f"out_{row0}" if tail_mode[0] else "out"], dma=f"o{tb}")
                out_ops.append(oi)

            stg_a.append(stage_a); stg_b.append(stage_b); stg_c.append(stage_c)
        if T == nt_run - 1:
            tail_mode[0] = True
            for f_ in stg_b + stg_c:
                f_()
        else:
            deferred.extend(stg_a + stg_b + stg_c)

    tail_mode[0] = True
    run_deferred(10 ** 6)
    S.emit(final_wait=out_ops)
    return nc


_NC_CACHE = {}


def _host_layout(inputs):
    f = lambda a: np.ascontiguousarray(np.asarray(a, dtype=np.float32))
    w_in = f(inputs["w_in"])
    b_in = f(inputs["b_in"])
    perm = np.array(PERM)
    wp = w_in[:, perm]
    Wh = f(wp.reshape(8, 128, NG, 512).transpose(2, 1, 0, 3).reshape(NG, 128, 4096))
    bp = b_in[perm]
    shared = {
        "Wh": Wh,
        "wco": f(f(inputs["w_conv_out"]).reshape(4, 128, 1024).transpose(1, 0, 2)),
        "wao": f(f(inputs["w_attn_out"]).reshape(4, 128, 1024).transpose(1, 0, 2)),
        "wo": f(f(inputs["w_o"]).reshape(8, 128, 1024).transpose(1, 0, 2)),
        "bcol": f(bp.reshape(44, 128).T),
        "bvb": f(b_in[2560:3072].reshape(1, 512)),
        "cw": f(f(inputs["conv_w"]).reshape(CW, 4, 128).transpose(2, 1, 0)),
        "cvec": f(np.concatenate([f(inputs["conv_b"]).reshape(4, 128).T,
                                  f(inputs["conv_ln_g"]).reshape(4, 128).T,
                                  f(inputs["conv_ln_b"]).reshape(4, 128).T], axis=1)),
        "bo": f(inputs["b_o"]).reshape(1, D),
        "og": f(inputs["out_ln_g"]).reshape(1, D),
        "ob": f(inputs["out_ln_b"]).reshape(1, D),
    }
    rb = f(inputs["rel_bias"])
    k = np.arange(128)[:, None]
    q = np.arange(128)[None, :]
    idx_prev = np.minimum(q - k + 128, 128) + 128
    idx_diag = (q - k) + 128
    BT = np.empty((128, 8, 256), np.float32)
    for h in range(8):
        BT[:, h, 0:128] = rb[h][idx_prev]
        BT[:, h, 128:256] = rb[h][idx_diag]
    shared["BT"] = BT
    shared["cfar"] = f(rb[:, 256].reshape(1, 8))
    return shared


def kernel(**inputs):
    x = np.asarray(inputs["x"], dtype=np.float32)
    ncores = x.shape[0]
    if "nc" not in _NC_CACHE:
        _NC_CACHE["nc"] = build_nc(NT)
    nc = _NC_CACHE["nc"]
    shared = _host_layout(inputs)
    in_maps = []
    for b in range(ncores):
        m = dict(shared)
        m["xtok"] = np.ascontiguousarray(x[b])
        m["xT"] = np.ascontiguousarray(x[b].T)
        in_maps.append(m)
    res = run_bass_kernel_spmd(nc, in_maps, core_ids=list(range(ncores)))
    out = np.stack([np.asarray(r["out"], dtype=np.float32) for r in res.results], axis=0)
    return out
```
